# Optimizing a Trainium2 kernel written in Bass

```python
import jax, jax.numpy as jnp
from jax import lax
import numpy as np

D_MODEL = 1024
BATCH = 4
SEQ = 4096
DEPTH = 1

FFN_HIDDEN = 2816
FFN_RESIDUAL_WEIGHT = 0.5

MLA_HEADS = 8
MLA_NOPE_DIM = 64
MLA_ROPE_DIM = 32
MLA_V_DIM = 64
Q_LORA_RANK = 192
KV_LORA_RANK = 128
ROPE_THETA = 10000.0
MAX_POS_OFFSET = 1024

FOX_HEADS = 8
FOX_HEAD_DIM = 64
FOX_WIDTH = FOX_HEADS * FOX_HEAD_DIM
FOX_BF_MIN = 1.0
FOX_BF_MAX = 6.0

MLA_WIDTH = MLA_HEADS * MLA_V_DIM
Q_BLOCK = 128
RMS_EPS = 1e-6

IN_SPLIT_SIZES = (Q_LORA_RANK, KV_LORA_RANK, MLA_ROPE_DIM,
                  FOX_WIDTH, FOX_WIDTH, FOX_WIDTH, FOX_HEADS,
                  D_MODEL, D_MODEL)
IN_WIDTH = sum(IN_SPLIT_SIZES)

kernel_name = "hybrid_mla_fox_macaron_gated"


def rms_norm(x, gain):
    xf = x.astype(jnp.float32)
    y = xf * lax.rsqrt(jnp.mean(xf * xf, axis=-1, keepdims=True) + RMS_EPS)
    return (y * gain.astype(jnp.float32)).astype(x.dtype)


def swiglu(x, w_gate, w_up, w_down):
    return (jax.nn.silu(x @ w_gate) * (x @ w_up)) @ w_down


def rope_tables(positions):
    half = MLA_ROPE_DIM // 2
    inv_freq = ROPE_THETA ** (-jnp.arange(half, dtype=jnp.float32) / half)
    ang = positions.astype(jnp.float32)[..., None] * inv_freq
    return jnp.cos(ang), jnp.sin(ang)


def apply_rotary(x, cos, sin):
    half = x.shape[-1] // 2
    xf = x.astype(jnp.float32)
    x1, x2 = xf[..., :half], xf[..., half:]
    return jnp.concatenate([x1 * cos - x2 * sin, x2 * cos + x1 * sin], axis=-1).astype(x.dtype)


def to_heads(t):
    return t.transpose(0, 2, 1, 3)


def from_heads(t):
    b, h, s, d = t.shape
    return t.transpose(0, 2, 1, 3).reshape(b, s, h * d)


def blocked_causal_attention(q, k, v, log_decay_cum=None):
    b, h, s, dk = q.shape
    dv = v.shape[-1]
    n_blocks = s // Q_BLOCK
    scale = dk ** -0.5
    k_pos = jnp.arange(s)

    def one_block(i):
        start = i * Q_BLOCK
        q_blk = lax.dynamic_slice_in_dim(q, start, Q_BLOCK, axis=2)
        logits = jnp.einsum('bhqd,bhkd->bhqk', q_blk, k,
                            preferred_element_type=jnp.float32) * scale
        if log_decay_cum is not None:
            c_q = lax.dynamic_slice_in_dim(log_decay_cum, start, Q_BLOCK, axis=2)
            logits = logits + (c_q[..., :, None] - log_decay_cum[..., None, :])
        q_pos = start + jnp.arange(Q_BLOCK)
        logits = jnp.where(k_pos[None, :] <= q_pos[:, None], logits, -jnp.inf)
        probs = jax.nn.softmax(logits, axis=-1)
        return jnp.einsum('bhqk,bhkd->bhqd', probs.astype(v.dtype), v)

    out = lax.map(one_block, jnp.arange(n_blocks))
    return out.transpose(1, 2, 0, 3, 4).reshape(b, h, s, dv)


def mla_mixer(q_lat, kv_lat, k_rope, cos, sin, q_lat_norm, w_qb, kv_lat_norm, w_kvb,
              q_nope_gain, q_rope_gain, k_nope_gain, k_rope_gain):
    b, s, _ = q_lat.shape
    q = (rms_norm(q_lat, q_lat_norm) @ w_qb).reshape(b, s, MLA_HEADS, MLA_NOPE_DIM + MLA_ROPE_DIM)
    q_nope = rms_norm(q[..., :MLA_NOPE_DIM], q_nope_gain)
    q_rope = apply_rotary(rms_norm(q[..., MLA_NOPE_DIM:], q_rope_gain),
                          cos[:, :, None, :], sin[:, :, None, :])
    kv = (rms_norm(kv_lat, kv_lat_norm) @ w_kvb).reshape(b, s, MLA_HEADS, MLA_NOPE_DIM + MLA_V_DIM)
    k_nope = rms_norm(kv[..., :MLA_NOPE_DIM], k_nope_gain)
    v = kv[..., MLA_NOPE_DIM:]
    k_r = apply_rotary(rms_norm(k_rope, k_rope_gain), cos, sin)
    k_r = jnp.broadcast_to(k_r[:, :, None, :], (b, s, MLA_HEADS, MLA_ROPE_DIM))
    q_full = jnp.concatenate([q_nope, q_rope], axis=-1)
    k_full = jnp.concatenate([k_nope, k_r], axis=-1)
    out = blocked_causal_attention(to_heads(q_full), to_heads(k_full), to_heads(v))
    return from_heads(out)


def fox_mixer(fq, fk, fv, f_logit, q_gain, k_gain, b_f):
    b, s, _ = fq.shape
    q = rms_norm(fq.reshape(b, s, FOX_HEADS, FOX_HEAD_DIM), q_gain)
    k = rms_norm(fk.reshape(b, s, FOX_HEADS, FOX_HEAD_DIM), k_gain)
    v = fv.reshape(b, s, FOX_HEADS, FOX_HEAD_DIM)
    log_f = jax.nn.log_sigmoid((f_logit + b_f).astype(jnp.float32))
    c = jnp.cumsum(log_f, axis=1).transpose(0, 2, 1)
    out = blocked_causal_attention(to_heads(q), to_heads(k), to_heads(v), c)
    return from_heads(out)


def setup_inputs(seed: int = 0) -> dict:
    key = jax.random.key(seed)
    ks = iter(jax.random.split(key, 40))
    f32 = jnp.float32

    def w(shape, fan_in):
        return jax.random.normal(next(ks), (DEPTH,) + shape, f32) * (fan_in ** -0.5)

    def gain(*shape):
        return 1.0 + 0.02 * jax.random.normal(next(ks), (DEPTH,) + shape, f32)

    x = jax.random.normal(next(ks), (BATCH, SEQ, D_MODEL), f32)
    offsets = jax.random.randint(next(ks), (BATCH, 1), 0, MAX_POS_OFFSET, dtype=jnp.int32)
    positions = (offsets + jnp.arange(SEQ, dtype=jnp.int32)[None, :]).astype(jnp.int32)
    return {
        "x": x,
        "positions": positions,
        "ffn1_norm": gain(D_MODEL),
        "ffn1_w_gate": w((D_MODEL, FFN_HIDDEN), D_MODEL),
        "ffn1_w_up": w((D_MODEL, FFN_HIDDEN), D_MODEL),
        "ffn1_w_down": w((FFN_HIDDEN, D_MODEL), FFN_HIDDEN),
        "mix_norm": gain(D_MODEL),
        "w_in": w((D_MODEL, IN_WIDTH), D_MODEL),
        "mla_q_lat_norm": gain(Q_LORA_RANK),
        "mla_w_qb": w((Q_LORA_RANK, MLA_HEADS * (MLA_NOPE_DIM + MLA_ROPE_DIM)), Q_LORA_RANK),
        "mla_kv_lat_norm": gain(KV_LORA_RANK),
        "mla_w_kvb": w((KV_LORA_RANK, MLA_HEADS * (MLA_NOPE_DIM + MLA_V_DIM)), KV_LORA_RANK),
        "mla_q_nope_gain": gain(MLA_NOPE_DIM),
        "mla_q_rope_gain": gain(MLA_ROPE_DIM),
        "mla_k_nope_gain": gain(MLA_NOPE_DIM),
        "mla_k_rope_gain": gain(MLA_ROPE_DIM),
        "fox_q_gain": gain(FOX_HEAD_DIM),
        "fox_k_gain": gain(FOX_HEAD_DIM),
        "fox_b_f": jax.random.uniform(next(ks), (DEPTH, FOX_HEADS), f32, FOX_BF_MIN, FOX_BF_MAX),
        "w_branch_mla": w((MLA_WIDTH, D_MODEL), MLA_WIDTH),
        "w_branch_fox": w((FOX_WIDTH, D_MODEL), FOX_WIDTH),
        "b_gate": 0.02 * jax.random.normal(next(ks), (DEPTH, 2, D_MODEL), f32),
        "w_o": w((D_MODEL, D_MODEL), D_MODEL),
        "ffn2_norm": gain(D_MODEL),
        "ffn2_w_gate": w((D_MODEL, FFN_HIDDEN), D_MODEL),
        "ffn2_w_up": w((D_MODEL, FFN_HIDDEN), D_MODEL),
        "ffn2_w_down": w((FFN_HIDDEN, D_MODEL), FFN_HIDDEN),
    }


def reference(x, positions, ffn1_norm, ffn1_w_gate, ffn1_w_up, ffn1_w_down, mix_norm, w_in,
              mla_q_lat_norm, mla_w_qb, mla_kv_lat_norm, mla_w_kvb, mla_q_nope_gain,
              mla_q_rope_gain, mla_k_nope_gain, mla_k_rope_gain, fox_q_gain, fox_k_gain,
              fox_b_f, w_branch_mla, w_branch_fox, b_gate, w_o, ffn2_norm, ffn2_w_gate,
              ffn2_w_up, ffn2_w_down):
    split_points = [int(p) for p in np.cumsum(IN_SPLIT_SIZES)[:-1]]
    cos, sin = rope_tables(positions)
    for l in range(DEPTH):
        x = x + FFN_RESIDUAL_WEIGHT * swiglu(rms_norm(x, ffn1_norm[l]),
                                             ffn1_w_gate[l], ffn1_w_up[l], ffn1_w_down[l])
        h = rms_norm(x, mix_norm[l])
        proj = h @ w_in[l]
        (q_lat, kv_lat, k_rope, fq, fk, fv, f_logit,
         g_mla, g_fox) = jnp.split(proj, split_points, axis=-1)
        y_mla = mla_mixer(q_lat, kv_lat, k_rope, cos, sin, mla_q_lat_norm[l], mla_w_qb[l],
                          mla_kv_lat_norm[l], mla_w_kvb[l], mla_q_nope_gain[l],
                          mla_q_rope_gain[l], mla_k_nope_gain[l], mla_k_rope_gain[l])
        y_fox = fox_mixer(fq, fk, fv, f_logit, fox_q_gain[l], fox_k_gain[l], fox_b_f[l])
        mixed = (jax.nn.sigmoid(g_mla + b_gate[l, 0]) * (y_mla @ w_branch_mla[l])
                 + jax.nn.sigmoid(g_fox + b_gate[l, 1]) * (y_fox @ w_branch_fox[l]))
        x = x + mixed @ w_o[l]
        x = x + FFN_RESIDUAL_WEIGHT * swiglu(rms_norm(x, ffn2_norm[l]),
                                             ffn2_w_gate[l], ffn2_w_up[l], ffn2_w_down[l])
    return x
```

```python
import contextlib
import os
import numpy as np
import concourse.bass as bass
import concourse.mybir as mybir
from concourse.bass_utils import run_bass_kernel_spmd

F32 = mybir.dt.float32
BF16 = mybir.dt.bfloat16
I32 = mybir.dt.int32
AF = mybir.ActivationFunctionType
ALU = mybir.AluOpType

D = 1024
S = 4096
HID = 2816
NHC = HID // 128
INW = 3944
EPS = 1e-6
NEG = -30000.0
C_QLAT, C_KVLAT, C_KROPE, C_FQ, C_FK, C_FV, C_FL, C_GM, C_GF = 0, 192, 320, 352, 864, 1376, 1888, 1896, 2920


class Tok:
    __slots__ = ("name", "last_w", "readers")

    def __init__(self, name):
        self.name = name
        self.last_w = None
        self.readers = {}


class Op:
    __slots__ = ("eng", "fn", "reads", "writes", "key", "grp", "deps", "sig", "idx", "semval", "sem", "grp_end")

    def __init__(self, eng, fn, reads, writes, key, grp):
        self.eng = eng
        self.fn = fn
        self.reads = reads
        self.writes = writes
        self.key = key
        self.grp = grp
        self.deps = ()
        self.sig = False
        self.sem = None
        self.semval = None
        self.grp_end = None


class Tile:
    def __init__(self, t, toks):
        self.t = t
        self.toks = tuple(toks)

    def __getitem__(self, k):
        return self.t[k]


def toks_of(*tiles):
    out = []
    for t in tiles:
        if isinstance(t, Tile):
            out.extend(t.toks)
        elif isinstance(t, Tok):
            out.append(t)
        else:
            out.extend(toks_of(*t))
    return out


class Prog:
    ENGS = ("pe", "act", "dve", "pool", "sp")

    def __init__(self, nc):
        self.nc = nc
        self.ops = []
        self.stack = contextlib.ExitStack()
        self.ntok = 0

    def tok(self, name=None):
        self.ntok += 1
        return Tok(name or f"t{self.ntok}")

    def op(self, eng, fn, reads=(), writes=(), key=None, grp=None):
        o = Op(eng, fn, tuple(toks_of(*reads)), tuple(toks_of(*writes)), key, grp)
        o.idx = len(self.ops)
        self.ops.append(o)
        return o

    def dma(self, eng, out, in_, reads=(), writes=(), key=None, grp=None):
        if isinstance(key, Tile):
            key = key.toks[0]
        assert key is not None
        return self.op(eng, lambda e: e.dma_start(out=out, in_=in_), reads, writes, key=key, grp=grp)

    def analyze(self):
        last_dma = {}
        for o in self.ops:
            deps = {}
            for t in o.reads:
                if t.last_w is not None:
                    deps[t.last_w.idx] = t.last_w
            for t in o.writes:
                if t.last_w is not None:
                    deps[t.last_w.idx] = t.last_w
                for r in t.readers.values():
                    deps[r.idx] = r
            if o.key is not None:
                pv = last_dma.get(id(o.key))
                if pv is not None and not (o.grp is not None and pv.grp == o.grp):
                    deps[pv.idx] = pv
                last_dma[id(o.key)] = o
            deps.pop(o.idx, None)
            dl = []
            for d in deps.values():
                if o.key is not None and d.key is o.key and o.grp is not None and d.grp == o.grp:
                    continue
                if d.eng == "pe" and o.eng == "pe" and d.key is None and o.key is None:
                    continue
                dl.append(d)
            o.deps = dl
            if o.key is not None:
                o.sig = True
            for d in dl:
                d.sig = True
            rk = o.eng if o.key is None else ("k", id(o.key))
            for t in o.reads:
                t.readers[rk] = o
            for t in o.writes:
                t.last_w = o
                t.readers = {}

    def emit(self, final_waits=()):
        nc = self.nc
        self.analyze()
        for o in final_waits:
            o.sig = True
        eng_sem = {e: self.stack.enter_context(nc.semaphore(f"s_{e}")) for e in self.ENGS}
        eng_cnt = {e: 0 for e in self.ENGS}
        by_key = {}
        for o in self.ops:
            if o.key is not None:
                by_key.setdefault(id(o.key), []).append(o)
        nks = 0
        for lst in by_key.values():
            if not any(o.sig for o in lst):
                continue
            sem = self.stack.enter_context(nc.semaphore(f"k{nks}"))
            nks += 1
            for n, o in enumerate(lst):
                o.sem = sem
                o.semval = 16 * (n + 1)
            i = len(lst) - 1
            while i >= 0:
                o = lst[i]
                if o.grp is None:
                    o.grp_end = o.semval
                    i -= 1
                else:
                    endv, g = o.semval, o.grp
                    while i >= 0 and lst[i].grp == g:
                        lst[i].grp_end = endv
                        i -= 1
        for o in self.ops:
            if o.key is None and o.sig:
                eng_cnt[o.eng] += 1
                o.sem = eng_sem[o.eng]
                o.semval = eng_cnt[o.eng]
                o.grp_end = o.semval
        self.nsem = nks + 5
        per_eng = {e: [o for o in self.ops if o.eng == e] for e in self.ENGS}
        fin = list(final_waits)

        def run(engname, eng):
            waited = {}

            def do_waits(deps):
                need = {}
                for d in deps:
                    v, s = d.grp_end, d.sem
                    if waited.get(id(s), 0) >= v:
                        continue
                    if need.get(id(s), (None, 0))[1] < v:
                        need[id(s)] = (s, v)
                for s, v in need.values():
                    eng.wait_ge(s, v)
                    waited[id(s)] = v

            for o in per_eng[engname]:
                do_waits(o.deps)
                if o.fn is None:
                    continue
                ins = o.fn(eng)
                if o.sig and o.sem is not None:
                    ins.then_inc(o.sem, 16 if o.key is not None else 1)
            if engname == "sp" and fin:
                do_waits(fin)

        with nc.Block() as block:
            @block.tensor
            def _(e):
                run("pe", e)

            @block.scalar
            def _(e):
                run("act", e)

            @block.vector
            def _(e):
                run("dve", e)

            @block.gpsimd
            def _(e):
                run("pool", e)

            @block.sync
            def _(e):
                run("sp", e)
        self.stack.close()


class Ring:
    def __init__(self, items):
        self.items = list(items)
        self.i = 0

    def next(self):
        it = self.items[self.i % len(self.items)]
        self.i += 1
        return it


class Arena:
    def __init__(self, p, nc, name, nbytes, slot):
        self.p = p
        self.nbytes = nbytes
        self.slot = slot
        self.t = nc.alloc_sbuf_tensor(name, [128, nbytes // 2], BF16)
        self.toks = [p.tok(f"{name}{i}") for i in range((nbytes + slot - 1) // slot)]

    def view(self, off, shape, dtype):
        esz = 2 if dtype == BF16 else 4
        n = int(np.prod(shape))
        nb = n * esz
        assert off % 4 == 0 and off + nb <= self.nbytes, (off, nb, self.nbytes)
        ap = self.t[:, off // 2:(off + nb) // 2]
        if dtype != BF16:
            ap = ap.bitcast(dtype)
        if len(shape) == 2:
            ap = ap.rearrange("p (a b) -> p a b", a=shape[0])
        elif len(shape) == 3:
            ap = ap.rearrange("p (a b c) -> p a b c", a=shape[0], b=shape[1])
        elif len(shape) == 4:
            ap = ap.rearrange("p (a b c d) -> p a b c d", a=shape[0], b=shape[1], c=shape[2])
        toks = self.toks[off // self.slot:(off + nb - 1) // self.slot + 1]
        t = Tile(ap, toks)
        t.arena, t.off = self, off
        return t

    def toks_range(self, off, nb):
        return self.toks[off // self.slot:(off + nb - 1) // self.slot + 1]


class Carver:
    def __init__(self, arena, base=0):
        self.arena = arena
        self.off = base

    def take(self, shape, dtype):
        esz = 2 if dtype == BF16 else 4
        nb = int(np.prod(shape)) * esz
        nb_al = (nb + 63) // 64 * 64
        if nb >= self.arena.slot:
            self.off = (self.off + self.arena.slot - 1) // self.arena.slot * self.arena.slot
        v = self.arena.view(self.off, shape, dtype)
        self.off += nb_al
        return v


def build_program(debug=False, stop_after=99):
    nc = bass.Bass("TRN2", target_bir_lowering=False)
    p = Prog(nc)

    def din(name, shape, dt=F32):
        return nc.dram_tensor(name, list(shape), dt, kind="ExternalInput").ap()

    skind = "ExternalOutput" if debug else "Internal"

    def dscr(name, shape, dt):
        return nc.dram_tensor(name, list(shape), dt, kind=skind).ap(), p.tok(name)

    x_d = din("x", [S, D])
    pos_d = din("pos", [1, S], I32)
    wg_d = [din("w_g1", [D, HID]), din("w_g2", [D, HID])]
    wu_d = [din("w_u1", [D, HID]), din("w_u2", [D, HID])]
    wd_d = [din("w_d1", [HID, D]), din("w_d2", [HID, D])]
    win_d = din("w_in", [D, INW])
    wqb_d = din("w_qb", [192, 768])
    wkvb_d = din("w_kvb", [128, 1024])
    wa_d = din("w_a", [512, D])
    wb_d = din("w_b", [512, D])
    wo_d = din("w_o", [D, D])
    cols_d = din("cols", [128, 64])
    cm_d = din("cmats", [128, 9, 128])
    out_d = nc.dram_tensor("out", [2048, D], F32, kind="ExternalOutput").ap()

    x1_s, t_x1s = dscr("x1_scr", [2048, D], F32)
    h2T_s, t_h2Ts = dscr("h2T_scr", [8, 128, S], BF16)
    Km_s, t_Ks = dscr("Km_scr", [8, 96, S], BF16)
    Kf_s, _ = dscr("Kf_scr", [8, 64, S], BF16)
    Qm_s, t_Qs = dscr("Qm_scr", [8, 96, 2048], BF16)
    Qf_s, _ = dscr("Qf_scr", [8, 64, 2048], BF16)
    LF_s, t_LFs = dscr("LF_scr", [8, S], F32)
    K_s = [Km_s, Kf_s]
    Q_s = [Qm_s, Qf_s]
    V_s, t_Vs = dscr("V_scr", [2, 8, 128, 32, 65], BF16)
    CK_s, t_CKs = dscr("CK_scr", [8, 3, S], BF16)
    CQ_s, t_CQs = dscr("CQ_scr", [8, 3, 2048], BF16)
    SG_s, t_SGs = dscr("SG_scr", [2, 8, 128, 2048], BF16)
    Y_s, t_Ys = dscr("Y_scr", [2, 8, 64, 2048], BF16)

    WA = Arena(p, nc, "wa", 135168, 512)
    STG = Arena(p, nc, "stg", 16384, 512)
    WK = Arena(p, nc, "wk", 56832, 1024)
    CST = Arena(p, nc, "cst", 4352, 4352)
    t_cst = CST.toks[0]
    cc = Carver(CST)
    cols = cc.take([64], F32)
    cmf = cc.take([3, 128], F32)
    cmb = cc.take([9, 128], BF16)
    negbf = cc.take([1], F32)
    epsc = cc.take([1], F32)
    onec = cc.take([1], F32)
    IDENT, ONES, B96, B128, R96, TRI, ROLE0, ROLE1, MCUM = range(9)

    PS = nc.alloc_psum_tensor("ps", [128, 4096], F32)
    tb = [p.tok(f"bank{i}") for i in range(8)]

    def bank(i, n=1):
        return Tile(PS[:, i * 512:(i + n) * 512], tb[i:i + n])

    def bank_bf(i):
        return Tile(PS[:, i * 512:(i + 1) * 512].bitcast(BF16), tb[i:i + 1])

    stg_ring = Ring([STG.view(i * 4096, [1024], F32) for i in range(4)])

    p.dma("sp", cols[:], cols_d, writes=[cols], key=t_cst, grp="c")
    st0 = stg_ring.next()
    for half, (a0, a1) in enumerate(((0, 5), (5, 9))):
        sth = stg_ring.next() if half else st0
        sthv = sth[:, 0:(a1 - a0) * 128].rearrange("p (a b) -> p a b", a=a1 - a0)
        p.dma("sp", sthv, cm_d[:, a0:a1, :], writes=[sth], key=sth)
        p.op("dve", lambda e, sthv=sthv, a0=a0, a1=a1: e.tensor_copy(out=cmb[:, a0:a1, :], in_=sthv), [sth], [cmb])
        for i_, j_ in ((0, 0), (1, 1), (2, 8)):
            if a0 <= j_ < a1:
                p.op("dve", lambda e, i_=i_, j_=j_, sthv=sthv, a0=a0: e.tensor_copy(out=cmf[:, i_, :], in_=sthv[:, j_ - a0, :]),
                     [sth], [cmf])
    FID, FONES, FMCUM = 0, 1, 2
    p.op("dve", lambda e: e.tensor_scalar(out=negbf[:], in0=cols[:, 31:32], scalar1=-1.0, scalar2=None, op0=ALU.mult),
         [cols], [negbf])
    p.op("dve", lambda e: e.memset(epsc[:], EPS), [], [epsc])
    p.op("dve", lambda e: e.memset(onec[:], 1.0), [], [onec])

    cast_engs = Ring(["dve", "act"])
    plain_engs = Ring(["pool", "dve", "act"])

    def load_w(*a_, **kw):
        for _ in load_w_gen(*a_, **kw):
            pass

    def load_w_gen(w_d, K, N, dst, gain_col0=None, engs=None, plain=None, dma_eng="sp", c_lo=0, c_hi=None):
        nkc = (K + 127) // 128
        npc = (N + 1023) // 1024
        wpc = (N + npc - 1) // npc
        for kc in range(nkc):
            rows = min(128, K - kc * 128)
            for c0 in range(0, N, wpc):
                if c0 < c_lo or (c_hi is not None and c0 >= c_hi):
                    continue
                w = min(wpc, N - c0)
                st = stg_ring.next()
                p.dma(dma_eng, st[0:rows, 0:w], w_d[kc * 128:kc * 128 + rows, c0:c0 + w], writes=[st], key=st)
                o = dst[0:rows, kc, c0:c0 + w]
                wt = dst
                if getattr(dst, "arena", None) is not None:
                    wt = dst.arena.toks_range(dst.off + (kc * N + c0) * 2, w * 2)
                if gain_col0 is not None:
                    g = cols[0:rows, gain_col0 + kc:gain_col0 + kc + 1]
                    en = (engs or cast_engs).next()
                    if en == "dve":
                        p.op("dve", lambda e, o=o, st=st, g=g, rows=rows, w=w: e.tensor_scalar(
                            out=o, in0=st[0:rows, 0:w], scalar1=g, scalar2=None, op0=ALU.mult), [st, cols], [wt])
                    else:
                        p.op("act", lambda e, o=o, st=st, g=g, rows=rows, w=w: e.activation(
                            out=o, in_=st[0:rows, 0:w], func=AF.Copy, scale=g), [st, cols], [wt])
                else:
                    en = (plain or plain_engs).next()
                    if en == "act":
                        p.op("act", lambda e, o=o, st=st, rows=rows, w=w: e.copy(out=o, in_=st[0:rows, 0:w]), [st], [wt])
                    else:
                        p.op(en, lambda e, o=o, st=st, rows=rows, w=w: e.tensor_copy(out=o, in_=st[0:rows, 0:w]), [st], [wt])
                yield None

    def ffn_views():
        wg = [WA.view(kc * 5632, [1, HID], BF16) for kc in range(8)]
        wu = [WA.view(45056 + kc * 5632, [1, HID], BF16) for kc in range(8)]
        wd = [WA.view(90112 + hc * 2048, [1, D], BF16) for hc in range(NHC)]
        return wg, wu, wd

    def load_ffn_gen(l, part="all", hcs=None, **kw):
        wg, wu, wd = ffn_views()
        g0 = 0 if l == 0 else 16
        if part in ("all", "gu"):
            for kc in range(8):
                yield from load_w_gen(wg_d[l][kc * 128:(kc + 1) * 128, :], 128, HID, wg[kc], gain_col0=g0 + kc, **kw)
                yield from load_w_gen(wu_d[l][kc * 128:(kc + 1) * 128, :], 128, HID, wu[kc], gain_col0=g0 + kc, **kw)
        if part in ("all", "d"):
            for hc in (hcs if hcs is not None else range(NHC)):
                yield from load_w_gen(wd_d[l][hc * 128:(hc + 1) * 128, :], 128, D, wd[hc], **kw)

    tr_banks = Ring([6, 7])

    def rms1(xt, hb, st):
        p.op("act", lambda e: e.activation(out=hb[:], in_=xt[:], func=AF.Square, accum_out=st[:, 0:1]), [xt], [hb, st])
        p.op("act", lambda e: e.activation(out=st[:, 1:2], in_=st[:, 0:1], func=AF.Ln, bias=epsc[:], scale=1.0 / D),
             [st, epsc], [st])
        p.op("act", lambda e: e.activation(out=st[:, 2:3], in_=st[:, 1:2], func=AF.Exp, scale=-0.5), [st], [st])
        p.op("dve", lambda e: e.tensor_scalar(out=hb[:], in0=xt[:], scalar1=st[:, 2:3], scalar2=None, op0=ALU.mult),
             [xt, st], [hb])

    def rms2(hb, hT, s):
        pb = bank_bf(tr_banks.next())
        for kc in range(8):
            p.op("pe", lambda e, kc=kc: e.transpose(pb[:, kc * 128:(kc + 1) * 128], hb[:, kc * 128:(kc + 1) * 128],
                                                     cmb[:, IDENT, :]), [hb, cmb], [pb])
        p.op("dve", lambda e: e.tensor_copy(out=hT[:, :, s * 128:(s + 1) * 128],
                                            in_=pb[:, :].rearrange("p (k t) -> p k t", k=8)), [pb], [hT])

    gu_banks = Ring([0, 1, 2])
    o_banks = Ring([3, 4, 5])

    def ffn_gateup(hT, actT, sg_ring, wg, wu, hcs):
        for hc in hcs:
            pb = bank(gu_banks.next())
            for which, wlist in ((0, wg), (1, wu)):
                for kc in range(8):
                    p.op("pe", lambda e, which=which, wl=wlist, kc=kc, hc=hc, pb=pb: e.matmul(
                        pb[:, which * 256:(which + 1) * 256], lhsT=wl[kc][:, 0, hc * 128:(hc + 1) * 128],
                        rhs=hT[:, kc, :], start=(kc == 0), stop=(kc == 7)), [wlist[kc], hT], [pb])
            sg = sg_ring.next()
            p.op("act", lambda e, pb=pb, sg=sg: e.activation(out=sg[:], in_=pb[:, 0:256], func=AF.Silu), [pb], [sg])
            p.op("dve", lambda e, pb=pb, sg=sg, hc=hc: e.tensor_tensor(out=actT[:, hc, :], in0=pb[:, 256:512], in1=sg[:],
                                                                        op=ALU.mult), [pb, sg], [actT])

    def ffn_down(xs, actT, wd):
        for s in range(2):
            for dh in range(2):
                pb = bank(o_banks.next())
                for hc in range(NHC):
                    p.op("pe", lambda e, s=s, dh=dh, hc=hc, pb=pb: e.matmul(
                        pb[:, :], lhsT=actT[:, hc, s * 128:(s + 1) * 128], rhs=wd[hc][:, 0, dh * 512:(dh + 1) * 512],
                        start=(hc == 0), stop=(hc == NHC - 1)), [actT, wd[hc]], [pb])
                p.op("dve", lambda e, s=s, dh=dh, pb=pb: e.scalar_tensor_tensor(
                    out=xs[s][:, dh * 512:(dh + 1) * 512], in0=pb[:, :], scalar=0.5, in1=xs[s][:, dh * 512:(dh + 1) * 512],
                    op0=ALU.mult, op1=ALU.add), [pb, xs[s]], [xs[s]])

    def ffn_pass(n_tiles, src_ap, src_tok, post1_fn, post2_fn, with_h2):
        wg, wu, wd = ffn_views()
        wc = Carver(WK)
        xs_ring = Ring([[wc.take([D], F32), wc.take([D], F32)] for _ in range(2)])
        hb_ring = Ring([wc.take([D], BF16) for _ in range(4 if with_h2 else 2)])
        hT_ring = Ring([wc.take([8, 256], BF16) for _ in range(2)])
        actT = wc.take([NHC, 256], BF16)
        sg_ring = Ring([wc.take([256], BF16) for _ in range(3)])
        st_ring = Ring([wc.take([4], F32) for _ in range(8)])
        h2T_ring = Ring([wc.take([8, 256], BF16) for _ in range(2)]) if with_h2 else None

        def pre0(t):
            xs = xs_ring.next()
            for s in range(2):
                r0 = t * 256 + s * 128
                p.dma("sp", xs[s][:], src_ap[r0:r0 + 128, :], reads=([src_tok] if src_tok else []), writes=[xs[s]], key=xs[s])
            return xs

        def pre1(xs):
            hbs = [hb_ring.next(), hb_ring.next()]
            for s in range(2):
                rms1(xs[s], hbs[s], st_ring.next())
            return hbs

        def pre2(hbs):
            hT = hT_ring.next()
            for s in range(2):
                rms2(hbs[s], hT, s)
            return hT

        xs = pre0(0)
        hT = pre2(pre1(xs))
        pend_post = None
        for t in range(n_tiles):
            if t + 1 < n_tiles:
                xs_n = pre0(t + 1)
            ffn_gateup(hT, actT, sg_ring, wg, wu, range(0, 4))
            if pend_post is not None:
                post2_fn(*pend_post)
                pend_post = None
            if t + 1 < n_tiles:
                hbs_n = pre1(xs_n)
            ffn_gateup(hT, actT, sg_ring, wg, wu, range(4, 12))
            if t + 1 < n_tiles:
                hT_n = pre2(hbs_n)
            ffn_gateup(hT, actT, sg_ring, wg, wu, range(12, NHC))
            ffn_down(xs, actT, wd)
            pend_post = post1_fn(t, xs, hb_ring, st_ring, h2T_ring)
            if t + 1 < n_tiles:
                xs, hT = xs_n, hT_n
        if pend_post is not None:
            post2_fn(*pend_post)

    for _ in load_ffn_gen(0):
        pass

    def p1_post1(t, xs, hb_ring, st_ring, h2T_ring):
        hbs = [hb_ring.next(), hb_ring.next()]
        for s in range(2):
            r0 = t * 256 + s * 128
            if t < 8:
                p.dma("sp", x1_s[r0:r0 + 128, :], xs[s][:], reads=[xs[s]], writes=[t_x1s], key=xs[s])
            rms1(xs[s], hbs[s], st_ring.next())
        return (t, hbs, h2T_ring)

    def p1_post2(t, hbs, h2T_ring):
        h2T = h2T_ring.next()
        for s in range(2):
            rms2(hbs[s], h2T, s)
        p.dma("sp", h2T_s[:, :, t * 256:(t + 1) * 256].rearrange("k p t -> p k t"), h2T[:], reads=[h2T],
              writes=[t_h2Ts], key=h2T)

    ffn_pass(16, x_d, None, p1_post1, p1_post2, True)

    if stop_after <= 1:
        p.emit(final_waits=[o for o in p.ops if o.key is not None and (t_h2Ts in o.writes or t_x1s in o.writes)])
        return nc

    win = [WA.view(kc * 7888, [1, INW], BF16) for kc in range(8)]
    wcar = Carver(WA, base=8 * 7888)
    wqb = wcar.take([2, 768], BF16)
    load_w(wqb_d, 192, 768, wqb, gain_col0=24)
    wkvb = wcar.take([1, 1024], BF16)
    load_w(wkvb_d, 128, 1024, wkvb, gain_col0=26)
    wkn2 = wcar.take([4, 128], BF16)
    p.op("dve", lambda e: e.tensor_copy(
        out=wkn2[:, :, :].rearrange("p j (i c) -> p j i c", i=2),
        in_=wkvb[:, 0, :].rearrange("p (j i c) -> p j i c", j=4, i=2)[:, :, :, 0:64]), [wkvb], [wkn2])
    wqn2 = wcar.take([2, 4, 128], BF16)
    wqr = wcar.take([2, 2, 128], BF16)
    for kc, rows in ((0, 128), (1, 64)):
        p.op("dve", lambda e, kc=kc, rows=rows: e.tensor_copy(
            out=wqn2[0:rows, kc, :, :].rearrange("p j (i c) -> p j i c", i=2),
            in_=wqb[0:rows, kc, :].rearrange("p (j i c) -> p j i c", j=4, i=2)[:, :, :, 0:64]), [wqb], [wqn2])
        p.op("dve", lambda e, kc=kc, rows=rows: e.tensor_copy(
            out=wqr[0:rows, kc, :, :].rearrange("p g (i c) -> p g i c", i=4),
            in_=wqb[0:rows, kc, :].rearrange("p (g i c) -> p g i c", g=2, i=4)[:, :, :, 64:96]), [wqb], [wqr])

    wc = Carver(WK)
    hT_ring = Ring([wc.take([8, 512], BF16) for _ in range(2)])
    posi = wc.take([512], I32)
    ang = wc.take([512], F32)
    tmpf = wc.take([512], F32)
    tmpi = posi
    cs_ring = Ring([(wc.take([512], F32), wc.take([512], F32)) for _ in range(2)])
    wfree = Carver(WA, base=(wcar.off + 511) // 512 * 512)
    raw_ring = Ring([wfree.take([512], F32) for _ in range(5)])
    sq_ring = Ring([wfree.take([512], BF16) for _ in range(5)])
    rs_ring = Ring([wfree.take([512], F32) for _ in range(5)])
    qn_ring = Ring([wfree.take([512], BF16) for _ in range(4)])
    t1_ring = Ring([wfree.take([512], F32) for _ in range(3)])
    t2_ring = Ring([wfree.take([512], F32) for _ in range(3)])
    ob_ring = Ring([wfree.take([512], BF16) for _ in range(6)])
    vt_ring = Ring([wfree.take([8, 4, 65], BF16) for _ in range(3)])
    kvn_ring = Ring([wc.take([512], BF16) for _ in range(2)])
    qn12_ring = Ring([(wc.take([512], BF16), wc.take([512], BF16)) for _ in range(2)])
    lf_ring = Ring([wc.take([512], F32) for _ in range(1)])
    pr = Ring(list(range(8)))
    for vt in vt_ring.items:
        p.op("pool", lambda e, vt=vt: e.memset(vt[:], 1.0), [], [vt])
    dq = Ring(["sp"])
    uniq = [0]

    def win_toks(kc, col0, n):
        return WA.toks_range(win[kc].off + col0 * 2, n * 2)

    def proj_fm(pb, nrow, col0, hT):
        for kc in range(8):
            p.op("pe", lambda e, kc=kc: e.matmul(pb[0:nrow, :], lhsT=win[kc][:, 0, col0:col0 + nrow], rhs=hT[:, kc, :],
                                                  start=(kc == 0), stop=(kc == 7)), [win_toks(kc, col0, nrow), hT], [pb])

    def stage_a(pb, nrow):
        raw, sq = raw_ring.next(), sq_ring.next()
        p.op("dve", lambda e: e.tensor_copy(out=raw[0:nrow, :], in_=pb[0:nrow, :]), [pb], [raw])
        p.op("pool", lambda e: e.tensor_tensor(out=sq[0:nrow, :], in0=raw[0:nrow, :], in1=raw[0:nrow, :], op=ALU.mult),
             [raw], [sq])
        return raw, sq

    def stage_b(sqs, bmat, nrow, scale):
        pb2 = bank(pr.next())
        rs = rs_ring.next()
        for i, (sq, kr) in enumerate(sqs):
            p.op("pe", lambda e, sq=sq, kr=kr, i=i: e.matmul(pb2[0:nrow, :], lhsT=cmb[0:kr, bmat, 0:nrow], rhs=sq[0:kr, :],
                                                           start=(i == 0), stop=(i == len(sqs) - 1)), [cmb, sq], [pb2])
        p.op("act", lambda e: e.activation(out=rs[0:nrow, :], in_=pb2[0:nrow, :], func=AF.Ln, bias=epsc[0:nrow, :], scale=scale),
             [pb2, epsc], [rs])
        p.op("act", lambda e: e.activation(out=rs[0:nrow, :], in_=rs[0:nrow, :], func=AF.Exp, scale=-0.5), [rs], [rs])
        return rs

    def stage_rope(qn, nrow, Ct, St, dsts):
        pb3 = bank(pr.next())
        p.op("pe", lambda e: e.matmul(pb3[0:nrow, :], lhsT=cmb[0:nrow, R96, 0:nrow], rhs=qn[0:nrow, :], start=True, stop=True),
             [cmb, qn], [pb3])
        t1, t2, ob = t1_ring.next(), t2_ring.next(), ob_ring.next()
        p.op("pool", lambda e: e.tensor_tensor(out=t1[0:nrow, :], in0=qn[0:nrow, :], in1=Ct[0:nrow, :], op=ALU.mult), [qn, Ct], [t1])
        p.op("dve", lambda e: e.tensor_tensor(out=t2[0:nrow, :], in0=pb3[0:nrow, :], in1=St[0:nrow, :], op=ALU.mult), [pb3, St], [t2])
        p.op("pool", lambda e: e.tensor_tensor(out=ob[0:nrow, :], in0=t1[0:nrow, :], in1=t2[0:nrow, :], op=ALU.add), [t1, t2], [ob])
        for (dst_ap, dst_tok, r0, r1) in dsts:
            p.dma(dq.next(), dst_ap, ob[r0:r1, :], reads=[ob], writes=[dst_tok], key=ob, grp=f"r{uniq[0]}")
        uniq[0] += 1

    def ch_normed(mm_fn, nrow, bmat, scale, gain_col, dsts, rope=None):
        pb = bank(pr.next())
        mm_fn(pb)
        raw, sq = stage_a(pb, nrow)
        yield
        yield
        rs = stage_b([(sq, nrow)], bmat, nrow, scale)
        if rope is None:
            ob = ob_ring.next()
            p.op("dve", lambda e: e.scalar_tensor_tensor(out=ob[0:nrow, :], in0=raw[0:nrow, :],
                                                          scalar=cols[0:nrow, gain_col:gain_col + 1], in1=rs[0:nrow, :],
                                                          op0=ALU.mult, op1=ALU.mult), [raw, rs, cols], [ob])
            for (dst_ap, dst_tok, r0, r1) in dsts:
                p.dma(dq.next(), dst_ap, ob[r0:r1, :], reads=[ob], writes=[dst_tok], key=ob, grp=f"o{uniq[0]}")
            uniq[0] += 1
            return
        qn = qn_ring.next()
        p.op("dve", lambda e: e.scalar_tensor_tensor(out=qn[0:nrow, :], in0=raw[0:nrow, :],
                                                      scalar=cols[0:nrow, gain_col:gain_col + 1], in1=rs[0:nrow, :],
                                                      op0=ALU.mult, op1=ALU.mult), [raw, rs, cols], [qn])
        yield
        yield
        stage_rope(qn, nrow, rope[0], rope[1], dsts)

    def ch_kvlat(hT, kvn):
        pb = bank(pr.next())
        proj_fm(pb, 128, C_KVLAT, hT)
        raw, sq = stage_a(pb, 128)
        yield
        yield
        rs = stage_b([(sq, 128)], ONES, 128, 1.0 / 128)
        p.op("dve", lambda e: e.tensor_tensor(out=kvn[:, :], in0=raw[:, :], in1=rs[:, :], op=ALU.mult), [raw, rs], [kvn])

    def ch_qlat(hT, qn1, qn2):
        pb1 = bank(pr.next())
        proj_fm(pb1, 128, C_QLAT, hT)
        raw1, sq1 = stage_a(pb1, 128)
        pb2_ = bank(pr.next())
        proj_fm(pb2_, 64, C_QLAT + 128, hT)
        raw2, sq2 = stage_a(pb2_, 64)
        yield
        yield
        rs = stage_b([(sq1, 128), (sq2, 64)], ONES, 128, 1.0 / 192)
        p.op("dve", lambda e: e.tensor_tensor(out=qn1[:, :], in0=raw1[:, :], in1=rs[:, :], op=ALU.mult), [raw1, rs], [qn1])
        p.op("dve", lambda e: e.tensor_tensor(out=qn2[0:64, :], in0=raw2[0:64, :], in1=rs[0:64, :], op=ALU.mult), [raw2, rs], [qn2])

    def ch_simple(fn):
        fn()
        return
        yield

    def run_chains(chains):
        active = []
        for g in chains:
            active.append(g)
            for a_ in reversed(list(active)):
                try:
                    next(a_)
                except StopIteration:
                    active.remove(a_)
        while active:
            for a_ in reversed(list(active)):
                try:
                    next(a_)
                except StopIteration:
                    active.remove(a_)

    TWO_PI = float(2 * np.pi)

    def sin_table(dst, phase):
        p.op("dve", lambda e: e.tensor_scalar(out=tmpf[:, :], in0=ang[:, :], scalar1=phase, scalar2=None, op0=ALU.add),
             [ang], [tmpf])
        p.op("dve", lambda e: e.tensor_copy(out=tmpi[:, :], in_=tmpf[:, :]), [tmpf], [tmpi])
        p.op("dve", lambda e: e.tensor_copy(out=dst[:, :], in_=tmpi[:, :]), [tmpi], [dst])
        p.op("dve", lambda e: e.tensor_tensor(out=tmpf[:, :], in0=tmpf[:, :], in1=dst[:, :], op=ALU.subtract), [tmpf, dst], [tmpf])
        p.op("dve", lambda e: e.tensor_scalar(out=dst[:, :], in0=tmpf[:, :], scalar1=0.5, scalar2=None, op0=ALU.is_ge), [tmpf], [dst])
        p.op("dve", lambda e: e.tensor_tensor(out=tmpf[:, :], in0=tmpf[:, :], in1=dst[:, :], op=ALU.subtract), [tmpf, dst], [tmpf])
        p.op("dve", lambda e: e.tensor_scalar(out=dst[:, :], in0=tmpf[:, :], scalar1=-0.5, scalar2=None, op0=ALU.is_lt), [tmpf], [dst])
        p.op("dve", lambda e: e.tensor_tensor(out=tmpf[:, :], in0=tmpf[:, :], in1=dst[:, :], op=ALU.add), [tmpf, dst], [tmpf])
        p.op("act", lambda e: e.activation(out=dst[:, :], in_=tmpf[:, :], func=AF.Sin, scale=TWO_PI), [tmpf], [dst])

    def tile_prologue(tt):
        tok0 = tt * 512
        hT = hT_ring.next()
        Ct, St = cs_ring.next()
        p.dma("sp", hT[:], h2T_s[:, :, tok0:tok0 + 512].rearrange("k p t -> p k t"), reads=[t_h2Ts], writes=[hT], key=hT)
        p.dma("pool", posi[:, :], pos_d[:, tok0:tok0 + 512].broadcast_to([128, 512]), writes=[posi], key=posi)
        p.op("dve", lambda e: e.tensor_copy(out=ang[:, :], in_=posi[:, :]), [posi], [ang])
        p.op("dve", lambda e: e.tensor_scalar(out=ang[:, :], in0=ang[:, :], scalar1=cols[:, 48:49], scalar2=None, op0=ALU.mult),
             [ang, cols], [ang])
        sin_table(St, 0.0)
        sin_table(Ct, 0.25)
        return hT, Ct, St

    def tile_chains(tt, hT, Ct, St):
        own = tt < 4
        tok0 = tt * 512
        ts = slice(tok0, tok0 + 512)
        kvn = kvn_ring.next()
        qn1, qn2 = qn12_ring.next()
        vtm, vtf = vt_ring.next(), vt_ring.next()
        chains = [ch_kvlat(hT, kvn)]
        if own:
            chains.append(ch_qlat(hT, qn1, qn2))
        chains.append(ch_normed(lambda pb: proj_fm(pb, 32, C_KROPE, hT), 32, ONES, 1.0 / 32, 51,
                                [(Km_s[h, 64:96, ts], t_Ks, 0, 32) for h in range(8)], rope=(Ct, St)))
        for pair in range(4):
            chains.append(ch_normed(lambda pb, pair=pair: proj_fm(pb, 128, C_FK + pair * 128, hT), 128, B128, 1.0 / 64, 30,
                                    [(Kf_s[2 * pair + i, :, ts], t_Ks, 64 * i, 64 * i + 64) for i in range(2)]))

        def flogit():
            pb = bank(pr.next())
            proj_fm(pb, 8, C_FL, hT)
            lf_t = lf_ring.next()
            p.op("act", lambda e: e.copy(out=lf_t[0:8, :], in_=pb[0:8, :]), [pb], [lf_t])
            p.dma("sp", LF_s[:, ts], lf_t[0:8, :], reads=[lf_t], writes=[t_LFs], key=lf_t)
        chains.append(ch_simple(flogit))

        def vfox(bl):
            pb = bank(pr.next())
            for kc in range(8):
                p.op("pe", lambda e, kc=kc: e.matmul(pb[:, :], lhsT=hT[:, kc, bl * 128:(bl + 1) * 128],
                                                      rhs=win[kc][:, 0, C_FV:C_FV + 512], start=(kc == 0), stop=(kc == 7)),
                     [hT, win_toks(kc, C_FV, 512)], [pb])
            p.op("dve", lambda e: e.tensor_copy(out=vtf[:, :, bl, 0:64], in_=pb[:, :].rearrange("p (h c) -> p h c", h=8)),
                 [pb], [vtf])
        for bl in range(4):
            chains.append(ch_simple(lambda bl=bl: vfox(bl)))
        chains.append(ch_simple(lambda: p.dma("sp", V_s[1, :, :, tt * 4:(tt + 1) * 4, :].rearrange("h p b c -> p h b c"), vtf[:],
                                              reads=[vtf], writes=[t_Vs], key=vtf)))
        if own:
            for pair in range(4):
                chains.append(ch_normed(lambda pb, pair=pair: proj_fm(pb, 128, C_FQ + pair * 128, hT), 128, B128, 1.0 / 64, 29,
                                        [(Qf_s[2 * pair + i, :, ts], t_Qs, 64 * i, 64 * i + 64) for i in range(2)]))
        for pair in range(4):
            def mm(pb, pair=pair):
                p.op("pe", lambda e: e.matmul(pb[:, :], lhsT=wkn2[:, pair, :], rhs=kvn[:, :], start=True, stop=True), [wkn2, kvn], [pb])
            chains.append(ch_normed(mm, 128, B128, 1.0 / 64, 28,
                                    [(Km_s[2 * pair + i, 0:64, ts], t_Ks, 64 * i, 64 * i + 64) for i in range(2)]))

        def vmla(bl):
            pb = bank(pr.next())
            p.op("pe", lambda e: e.matmul(pb[:, :].rearrange("p (h c) -> p h c", h=8), lhsT=kvn[:, bl * 128:(bl + 1) * 128],
                                          rhs=wkvb[:, 0, :].rearrange("p (h c) -> p h c", h=8)[:, :, 64:128], start=True, stop=True),
                 [kvn, wkvb], [pb])
            p.op("act", lambda e: e.copy(out=vtm[:, :, bl, 0:64], in_=pb[:, :].rearrange("p (h c) -> p h c", h=8)), [pb], [vtm])
        for bl in range(4):
            chains.append(ch_simple(lambda bl=bl: vmla(bl)))
        chains.append(ch_simple(lambda: p.dma("sp", V_s[0, :, :, tt * 4:(tt + 1) * 4, :].rearrange("h p b c -> p h b c"), vtm[:],
                                              reads=[vtm], writes=[t_Vs], key=vtm)))
        if own:
            def gate(g, c0, dc):
                pb = bank(pr.next())
                proj_fm(pb, 128, c0 + dc * 128, hT)
                ob = ob_ring.next()
                p.op("act", lambda e: e.activation(out=ob[:, :], in_=pb[:, :], func=AF.Sigmoid,
                                                   bias=cols[:, 32 + 8 * g + dc:33 + 8 * g + dc]), [pb, cols], [ob])
                p.dma(dq.next(), SG_s[g, dc, :, ts], ob[:, :], reads=[ob], writes=[t_SGs], key=ob)
            gl = [(g, c0, dc) for g, c0 in ((0, C_GM), (1, C_GF)) for dc in range(8)]
            qch = []
            for pair in range(4):
                def mmq(pb, pair=pair):
                    p.op("pe", lambda e: e.matmul(pb[:, :], lhsT=wqn2[:, 0, pair, :], rhs=qn1[:, :], start=True, stop=False),
                         [wqn2, qn1], [pb])
                    p.op("pe", lambda e: e.matmul(pb[:, :], lhsT=wqn2[0:64, 1, pair, :], rhs=qn2[0:64, :], start=False, stop=True),
                         [wqn2, qn2], [pb])
                qch.append(ch_normed(mmq, 128, B128, 1.0 / 64, 27,
                                     [(Qm_s[2 * pair + i, 0:64, ts], t_Qs, 64 * i, 64 * i + 64) for i in range(2)]))
            for g4 in range(2):
                def mmr(pb, g4=g4):
                    p.op("pe", lambda e: e.matmul(pb[:, :], lhsT=wqr[:, 0, g4, :], rhs=qn1[:, :], start=True, stop=False),
                         [wqr, qn1], [pb])
                    p.op("pe", lambda e: e.matmul(pb[:, :], lhsT=wqr[0:64, 1, g4, :], rhs=qn2[0:64, :], start=False, stop=True),
                         [wqr, qn2], [pb])
                qch.append(ch_normed(mmr, 128, B96, 1.0 / 32, 50,
                                     [(Qm_s[4 * g4 + i, 64:96, ts], t_Qs, 32 * i, 32 * i + 32) for i in range(4)], rope=(Ct, St)))
            chains.extend(qch)
            for a_ in gl:
                chains.append(ch_simple(lambda a_=a_: gate(*a_)))
        return chains

    nxt = tile_prologue(0)
    wq_ = Ring(["sp", "pool"])
    for c0 in range(0, INW, 986):
        for kc in range(8):
            load_w(win_d[kc * 128:(kc + 1) * 128, :], 128, INW, win[kc], gain_col0=8 + kc, c_lo=c0, c_hi=c0 + 986,
                   dma_eng=wq_.next())
    for tt in range(8):
        hT, Ct, St = nxt
        if tt + 1 < 8:
            nxt = tile_prologue(tt + 1)
        run_chains(tile_chains(tt, hT, Ct, St))

    def cumsum_gen():
        wc2 = Carver(WA, base=0)
        BA = wc2.take([S], F32)
        BB = wc2.take([S], F32)
        CSP = wc2.take([3, S], BF16)
        sm = wc2.take([256], F32)
        tT = wc2.take([8], F32)
        p.dma("sp", BA[0:8, :], LF_s[:, :], reads=[t_LFs], writes=[BA], key=BA)
        yield
        yield
        p.op("act", lambda e: e.activation(out=BA[0:8, :], in_=BA[0:8, :], func=AF.Exp, bias=negbf[0:8, :], scale=-1.0),
             [BA, negbf], [BA])
        p.op("act", lambda e: e.activation(out=BA[0:8, :], in_=BA[0:8, :], func=AF.Ln, bias=onec[0:8, :], scale=1.0), [BA, onec], [BA])
        yield
        p.op("pool", lambda e: e.memset(BB[0:8, :], 1.0), [], [BB])
        yield
        p.op("dve", lambda e: e.tensor_tensor_scan(out=BB[0:8, :], data0=BB[0:8, :], data1=BA[0:8, :], initial=0.0,
                                                   op0=ALU.mult, op1=ALU.add), [BB, BA], [BB])
        SCv = BB[0:8, :].rearrange("p (b t) -> p b t", b=32)
        yield
        p.op("dve", lambda e: e.tensor_copy(out=sm[0:8, 0:32], in_=SCv[:, :, 127]), [BB], [sm])
        yield
        p.op("dve", lambda e: e.tensor_copy(out=sm[0:8, 32:33], in_=sm[0:8, 0:1]), [sm], [sm])
        yield
        p.op("dve", lambda e: e.tensor_tensor(out=sm[0:8, 33:64], in0=sm[0:8, 1:32], in1=sm[0:8, 0:31], op=ALU.subtract), [sm], [sm])
        yield
        p.op("dve", lambda e: e.tensor_tensor(out=sm[0:8, 96:128], in0=sm[0:8, 0:32], in1=sm[0:8, 32:64], op=ALU.subtract), [sm], [sm])
        yield
        pbt = bank(7)
        yield
        p.op("pe", lambda e: e.transpose(pbt[0:32, 0:8], sm[0:8, 32:64], cmf[0:8, FID, 0:8]), [sm, cmf], [pbt])
        yield
        p.op("dve", lambda e: e.tensor_copy(out=tT[0:32, :], in_=pbt[0:32, 0:8]), [pbt], [tT])
        yield
        pbo = bank(7)
        yield
        p.op("pe", lambda e: e.matmul(pbo[0:8, 0:32], lhsT=tT[0:32, :], rhs=cmf[0:32, FMCUM, 0:32], start=True, stop=True),
             [tT, cmf], [pbo])
        p.op("dve", lambda e: e.tensor_tensor(out=sm[0:8, 64:96], in0=pbo[0:8, 0:32], in1=sm[0:8, 96:128], op=ALU.subtract),
             [pbo, sm], [sm])
        yield "pe_done"
        p.op("dve", lambda e: e.tensor_tensor(out=SCv, in0=SCv, in1=sm[0:8, 64:96].unsqueeze(2).broadcast_to([8, 32, 128]),
                                              op=ALU.add), [BB, sm], [BB])
        p.op("dve", lambda e: e.tensor_scalar(out=BB[0:8, :], in0=BB[0:8, :], scalar1=8.0, scalar2=None, op0=ALU.mult), [BB], [BB])
        yield
        yield
        p.op("dve", lambda e: e.tensor_copy(out=CSP[0:8, 0, :], in_=BB[0:8, :]), [BB], [CSP])
        yield
        p.op("dve", lambda e: e.tensor_tensor(out=BA[0:8, :], in0=BB[0:8, :], in1=CSP[0:8, 0, :], op=ALU.subtract), [BB, CSP], [BA])
        yield
        p.op("dve", lambda e: e.tensor_copy(out=CSP[0:8, 1, :], in_=BA[0:8, :]), [BA], [CSP])
        yield
        p.op("dve", lambda e: e.tensor_tensor(out=BA[0:8, :], in0=BA[0:8, :], in1=CSP[0:8, 1, :], op=ALU.subtract), [BA, CSP], [BA])
        yield
        p.op("dve", lambda e: e.tensor_copy(out=CSP[0:8, 2, :], in_=BA[0:8, :]), [BA], [CSP])
        yield
        p.dma("sp", CK_s[:, :, :], CSP[0:8, :, :], reads=[CSP], writes=[t_CKs], key=CSP)
        yield
        p.op("dve", lambda e: e.tensor_scalar(out=CSP[0:8, :, :], in0=CSP[0:8, :, :], scalar1=-1.0, scalar2=None, op0=ALU.mult),
             [CSP], [CSP])
        p.dma("sp", CQ_s[:, :, :], CSP[0:8, :, 0:2048], reads=[CSP], writes=[t_CQs], key=CSP)
        yield

    cumgen = cumsum_gen()
    if stop_after <= 2:
        for _ in cumgen:
            pass
    if stop_after <= 2:
        p.emit(final_waits=[o for o in p.ops if o.key is not None and any(
            t in o.writes for t in (t_Ks, t_Qs, t_Vs, t_CKs, t_CQs, t_SGs, t_h2Ts, t_x1s))])
        return nc

    wc = Carver(WK)
    k_ring = Ring([wc.take([S], BF16) for _ in range(2)])
    q_ring = Ring([wc.take([2048], BF16) for _ in range(2)])
    v_ring = Ring([wc.take([32, 65], BF16) for _ in range(2)])
    pt_ring = Ring([wc.take([2, 512], BF16) for _ in range(4)])
    num_ring = Ring([wc.take([512], F32) for _ in range(3)])
    rc_ring = Ring([wc.take([512], F32) for _ in range(2)])
    y_ring = Ring([wc.take([512], BF16) for _ in range(2)])
    s_tiles = Ring([bank(0, 2), bank(2, 2), bank(4, 2)])
    po_ring = Ring([bank(6), bank(7)])

    def load_head(m, h):
        kt, qt, vt = k_ring.next(), q_ring.next(), v_ring.next()
        kd = 96 if m == 0 else 64
        if m == 1:
            p.op("pool", lambda e: e.memset(kt[64:96, :], 0.0), [], [kt])
            p.op("pool", lambda e: e.memset(qt[64:96, :], 0.0), [], [qt])
            p.op("pool", lambda e: e.memset(kt[64:70, :], 1.0), [], [kt])
            p.op("pool", lambda e: e.memset(qt[64:70, :], 1.0), [], [qt])
        p.dma("sp", kt[0:kd, :], K_s[m][h, 0:kd, :], reads=[t_Ks], writes=[kt], key=kt, grp=f"k{m}{h}")
        p.dma("sp", qt[0:kd, :], Q_s[m][h, 0:kd, :], reads=[t_Qs], writes=[qt], key=qt, grp=f"q{m}{h}")
        p.dma("sp", vt[:, :, :], V_s[m, h, :, :, :], reads=[t_Vs], writes=[vt], key=vt)
        if m == 1:
            p.dma("sp", kt[67:70, :], CK_s[h, :, :], reads=[t_CKs], writes=[kt], key=kt, grp=f"k{m}{h}")
            p.dma("sp", qt[64:67, :], CQ_s[h, :, :], reads=[t_CQs], writes=[qt], key=qt, grp=f"q{m}{h}")
        return kt, qt, vt

    steps = []
    heads = [(m, h) for m in [int(c) for c in os.environ.get("ATT_M", "01")] for h in range(int(os.environ.get("ATT_H", "8")))]
    for (m, h) in heads:
        for G in range(4):
            nj = 4 * G + 4
            for j in range(nj):
                steps.append(dict(m=m, h=h, G=G, j=j, first=(j == 0), last=(j == nj - 1)))

    head_tiles = {}
    state = {}
    pending = []

    def emit_qk(st):
        m, h, G, j = st["m"], st["h"], st["G"], st["j"]
        if (m, h) not in head_tiles:
            head_tiles[(m, h)] = load_head(m, h)
        kt, qt, vt = head_tiles[(m, h)]
        kd = 96
        off = 0 if j < 4 * G else (j - 4 * G) * 128
        stile = s_tiles.next()
        st["stile"] = stile
        st["off"] = off
        q0 = 512 * G + off
        for side in range(2):
            kb = j + 16 * side
            diag = j >= 4 * G
            p.op("pe", lambda e, side=side, kb=kb, diag=diag: e.matmul(
                stile[:, side * 512 + off:(side + 1) * 512], lhsT=kt[0:kd, kb * 128:(kb + 1) * 128],
                rhs=qt[0:kd, q0:512 * G + 512], start=True, stop=(not diag)), [kt, qt], [stile])
            if diag:
                mk = TRI if side == 0 else (ROLE0 if j % 2 == 0 else ROLE1)
                p.op("pe", lambda e, side=side, mk=mk: e.matmul(
                    stile[:, side * 512 + off:side * 512 + off + 128], lhsT=cmb[:, IDENT, :], rhs=cmb[:, mk, :],
                    start=False, stop=True), [cmb], [stile])

    def emit_exp_pv(st):
        m, h, G, j = st["m"], st["h"], st["G"], st["j"]
        kt, qt, vt = head_tiles[(m, h)]
        stile, off = st["stile"], st["off"]
        first, last = st["first"], st["last"]
        pt = pt_ring.next()
        scale = 96 ** -0.5 if m == 0 else 0.125
        sv = stile[:, :].rearrange("p (s q) -> p s q", s=2)
        p.op("act", lambda e: e.activation(out=pt[:, :, off:512], in_=sv[:, :, off:512], func=AF.Exp, scale=scale), [stile], [pt])
        if first:
            state["po"] = po_ring.next()
        po = state["po"]
        for side in range(2):
            kb = j + 16 * side
            p.op("pe", lambda e, side=side, kb=kb: e.matmul(
                po[0:65, off:512], lhsT=vt[:, kb, :], rhs=pt[:, side, off:512],
                start=(first and side == 0), stop=(last and side == 1)), [vt, pt], [po])
        if last:
            num, y = num_ring.next(), y_ring.next()
            p.op("act", lambda e: e.copy(out=num[0:65, :], in_=po[0:65, :]), [po], [num])

            def norm():
                p.op("pe", lambda e: e.matmul(po[0:64, 0:512], lhsT=cmf[64:65, FONES, 0:64], rhs=num[64:65, :], start=True, stop=True),
                     [cmf, num], [po])
                rc = rc_ring.next()
                p.op("dve", lambda e: e.reciprocal(out=rc[0:64, :], in_=po[0:64, 0:512]), [po], [rc])
                p.op("dve", lambda e: e.tensor_tensor(out=y[0:64, :], in0=num[0:64, :], in1=rc[0:64, :], op=ALU.mult),
                     [rc, num], [y])
                p.dma("sp", Y_s[m, h, :, 512 * G:512 * G + 512], y[0:64, :], reads=[y], writes=[t_Ys], key=y)
            pending.append([2, norm])

    wab = [WA.view(90112, [4, D], BF16), WA.view(98304, [4, D], BF16)]
    wo = WA.view(106496, [8, D], BF16)

    def w3_all():
        kw = dict(engs=Ring(["dve"]), plain=Ring(["pool"]), dma_eng="pool")
        yield from load_w_gen(wa_d, 512, D, wab[0], plain=Ring(["pool"]), dma_eng="pool")
        yield from load_w_gen(wb_d, 512, D, wab[1], plain=Ring(["pool"]), dma_eng="pool")
        yield from load_w_gen(wo_d, D, D, wo, plain=Ring(["pool"]), dma_eng="pool")
        yield from load_ffn_gen(1, part="gu", **kw)
        yield from load_ffn_gen(1, part="d", hcs=range(16, NHC), **kw)
    w3gen = w3_all()
    if stop_after > 2:
        while next(cumgen) != "pe_done":
            pass
    LOOK = 2
    for i in range(min(LOOK, len(steps))):
        emit_qk(steps[i])
    for i, st in enumerate(steps):
        if i + LOOK < len(steps):
            emit_qk(steps[i + LOOK])
        for pe_ in list(pending):
            pe_[0] -= 1
            if pe_[0] <= 0:
                pe_[1]()
                pending.remove(pe_)
        emit_exp_pv(st)
        if st["first"] and st["G"] == 0:
            hi = heads.index((st["m"], st["h"]))
            if hi + 1 < len(heads) and heads[hi + 1] not in head_tiles:
                head_tiles[heads[hi + 1]] = load_head(*heads[hi + 1])
        if next(cumgen, "done") == "done" and i >= 48 and i % 6 == 0 and not os.environ.get("NO_W3"):
            next(w3gen, None)
    for pe_ in pending:
        pe_[1]()
    for _ in cumgen:
        pass
    if not os.environ.get("NO_W3"):
        for _ in w3gen:
            pass

    if stop_after <= 3:
        p.emit(final_waits=[o for o in p.ops if o.key is not None and any(
            t in o.writes for t in (t_Ys, t_Ks, t_Qs, t_Vs, t_CKs, t_CQs, t_SGs, t_h2Ts, t_x1s))])
        return nc

    wc = Carver(WK)
    xs_ring = Ring([[wc.take([D], F32), wc.take([D], F32)] for _ in range(3)])
    mix_ring = Ring([wc.take([8, 256], BF16) for _ in range(2)])
    sgt_ring = Ring([[wc.take([8, 256], BF16), wc.take([8, 256], BF16)] for _ in range(2)])
    sc = Carver(STG)
    ym_ring = Ring([[sc.take([4, 256], BF16), sc.take([4, 256], BF16)] for _ in range(2)])
    ta_ring = Ring([sc.take([256], BF16) for _ in range(3)])
    tb_ring = Ring([sc.take([256], BF16) for _ in range(3)])

    def pa_loads(t):
        c0 = t * 256
        xs, ym, sg = xs_ring.next(), ym_ring.next(), sgt_ring.next()
        for m in range(2):
            p.dma("sp", ym[m][:, :, :], Y_s[m, :, :, c0:c0 + 256].rearrange("(pr j) r t -> (j r) pr t", j=2),
                  reads=[t_Ys], writes=[ym[m]], key=ym[m])
            p.dma("sp", sg[m][:, :, :], SG_s[m, :, :, c0:c0 + 256].rearrange("d p t -> p d t"),
                  reads=[t_SGs], writes=[sg[m]], key=sg[m])
        for s in range(2):
            p.dma("sp", xs[s][:], x1_s[c0 + s * 128:c0 + (s + 1) * 128, :], reads=[t_x1s], writes=[xs[s]], key=xs[s])
        return xs, ym, sg

    def pa_branch(ym, sg):
        mixT = mix_ring.next()
        for dc in range(8):
            pb = bank(gu_banks.next())
            for m in range(2):
                for pr_ in range(4):
                    p.op("pe", lambda e, m=m, pr_=pr_, dc=dc, pb=pb: e.matmul(
                        pb[:, m * 256:(m + 1) * 256], lhsT=wab[m][:, pr_, dc * 128:(dc + 1) * 128], rhs=ym[m][:, pr_, :],
                        start=(pr_ == 0), stop=(pr_ == 3)), [wab[m], ym[m]], [pb])
            ta, tb_ = ta_ring.next(), tb_ring.next()
            p.op("dve", lambda e, pb=pb, ta=ta, dc=dc: e.tensor_tensor(out=ta[:, :], in0=pb[:, 0:256], in1=sg[0][:, dc, :],
                                                                    op=ALU.mult), [pb, sg[0]], [ta])
            p.op("dve", lambda e, pb=pb, tb_=tb_, dc=dc: e.tensor_tensor(out=tb_[:, :], in0=pb[:, 256:512], in1=sg[1][:, dc, :],
                                                                      op=ALU.mult), [pb, sg[1]], [tb_])
            p.op("pool", lambda e, ta=ta, tb_=tb_, dc=dc: e.tensor_tensor(out=mixT[:, dc, :], in0=ta[:, :], in1=tb_[:, :],
                                                                       op=ALU.add), [ta, tb_], [mixT])
        return mixT

    def pa_wo(t, xs, mixT):
        c0 = t * 256
        for s in range(2):
            for dh in range(2):
                pb = bank(o_banks.next())
                for dc in range(8):
                    p.op("pe", lambda e, s=s, dh=dh, dc=dc, pb=pb: e.matmul(
                        pb[:, :], lhsT=mixT[:, dc, s * 128:(s + 1) * 128], rhs=wo[:, dc, dh * 512:(dh + 1) * 512],
                        start=(dc == 0), stop=(dc == 7)), [mixT, wo], [pb])
                p.op("dve", lambda e, s=s, dh=dh, pb=pb: e.tensor_tensor(
                    out=xs[s][:, dh * 512:(dh + 1) * 512], in0=pb[:, :], in1=xs[s][:, dh * 512:(dh + 1) * 512], op=ALU.add),
                    [pb, xs[s]], [xs[s]])
            p.dma("sp", x1_s[c0 + s * 128:c0 + (s + 1) * 128, :], xs[s][:], reads=[xs[s]], writes=[t_x1s], key=xs[s])

    NA = 8
    ld = [pa_loads(0), pa_loads(1)]
    mixes = [pa_branch(ld[0][1], ld[0][2])]
    for t in range(NA):
        if t + 2 < NA:
            ld.append(pa_loads(t + 2))
        if t + 1 < NA:
            mixes.append(pa_branch(ld[t + 1][1], ld[t + 1][2]))
        pa_wo(t, ld[t][0], mixes[t])
    for _ in load_ffn_gen(1, part="d", hcs=range(0, 16), plain=Ring(["pool", "dve"]), dma_eng="sp"):
        pass

    outs = []

    def p3_post1(t, xs, hb_ring, st_ring, h2T_ring):
        for s in range(2):
            r0 = t * 256 + s * 128
            outs.append(p.dma("sp", out_d[r0:r0 + 128, :], xs[s][:], reads=[xs[s]], key=xs[s]))
        return None

    ffn_pass(8, x1_s, t_x1s, p3_post1, None, False)
    p.emit(final_waits=outs)
    return nc


def _role_blocks(role):
    own = []
    for m in range(8):
        own += [4 * m, 4 * m + 3] if role == 0 else [4 * m + 1, 4 * m + 2]
    oth = [g for g in range(32) if g not in own]
    oth = []
    for m in range(8):
        oth += [4 * m + 1, 4 * m + 2] if role == 0 else [4 * m, 4 * m + 3]
    return own, oth


def _const_mats(role):
    cm = np.zeros((128, 9, 128), np.float32)
    cm[:, 0, :] = np.eye(128, dtype=np.float32)
    cm[:, 1, :] = 1.0
    for g in range(4):
        cm[32 * g:32 * g + 32, 2, 32 * g:32 * g + 32] = 1.0
    cm[0:64, 3, 0:64] = 1.0
    cm[64:128, 3, 64:128] = 1.0
    for g in range(4):
        for i in range(16):
            cm[32 * g + 16 + i, 4, 32 * g + i] = -1.0
            cm[32 * g + i, 4, 32 * g + 16 + i] = 1.0
    k = np.arange(128)[:, None]
    q = np.arange(128)[None, :]
    cm[:, 5, :] = np.where(k > q, NEG, 0.0)
    cm[:, 6, :] = NEG if role == 0 else 0.0
    cm[:, 7, :] = 0.0 if role == 0 else NEG
    own, oth = _role_blocks(role)
    glob = np.array(own + oth)
    cm[0:32, 8, 0:32] = (glob[:, None] < glob[None, :]).astype(np.float32)
    return cm


_NC_CACHE = {}


def _prep_inputs(x, positions, ffn1_norm, ffn1_w_gate, ffn1_w_up, ffn1_w_down, mix_norm, w_in,
                 mla_q_lat_norm, mla_w_qb, mla_kv_lat_norm, mla_w_kvb, mla_q_nope_gain,
                 mla_q_rope_gain, mla_k_nope_gain, mla_k_rope_gain, fox_q_gain, fox_k_gain,
                 fox_b_f, w_branch_mla, w_branch_fox, b_gate, w_o, ffn2_norm, ffn2_w_gate,
                 ffn2_w_up, ffn2_w_down):
    f = lambda a: np.ascontiguousarray(np.asarray(a, dtype=np.float32))
    x = f(x)
    positions = np.asarray(positions).astype(np.int32)
    cols = np.zeros((128, 64), np.float32)
    cols[:, 0:8] = f(ffn1_norm)[0].reshape(8, 128).T
    cols[:, 8:16] = f(mix_norm)[0].reshape(8, 128).T
    cols[:, 16:24] = f(ffn2_norm)[0].reshape(8, 128).T
    cols[:, 24] = f(mla_q_lat_norm)[0][0:128]
    cols[0:64, 25] = f(mla_q_lat_norm)[0][128:192]
    cols[:, 26] = f(mla_kv_lat_norm)[0]
    cols[:, 27] = np.tile(f(mla_q_nope_gain)[0], 2)
    cols[:, 28] = np.tile(f(mla_k_nope_gain)[0], 2)
    cols[:, 50] = np.tile(f(mla_q_rope_gain)[0], 4)
    cols[0:32, 51] = f(mla_k_rope_gain)[0]
    cols[:, 29] = np.tile(f(fox_q_gain)[0], 2)
    cols[:, 30] = np.tile(f(fox_k_gain)[0], 2)
    cols[0:8, 31] = f(fox_b_f)[0]
    cols[:, 32:40] = f(b_gate)[0, 0].reshape(8, 128).T
    cols[:, 40:48] = f(b_gate)[0, 1].reshape(8, 128).T
    inv_freq = (np.float32(10000.0) ** (-np.arange(16, dtype=np.float32) / np.float32(16))).astype(np.float32)
    cols[:, 48] = np.tile(inv_freq / np.float32(2 * np.pi), 8)
    shared = {
        "w_g1": f(ffn1_w_gate)[0], "w_u1": f(ffn1_w_up)[0], "w_d1": f(ffn1_w_down)[0],
        "w_g2": f(ffn2_w_gate)[0], "w_u2": f(ffn2_w_up)[0], "w_d2": f(ffn2_w_down)[0],
        "w_in": f(w_in)[0], "w_qb": f(mla_w_qb)[0], "w_kvb": f(mla_w_kvb)[0],
        "w_a": f(w_branch_mla)[0], "w_b": f(w_branch_fox)[0], "w_o": f(w_o)[0], "cols": cols,
    }
    in_maps = []
    orders = []
    for c in range(8):
        b, role = c // 2, c % 2
        own, oth = _role_blocks(role)
        order = np.concatenate([np.arange(g * 128, (g + 1) * 128) for g in own + oth])
        orders.append((b, np.concatenate([np.arange(g * 128, (g + 1) * 128) for g in own])))
        m = dict(shared)
        m["x"] = np.ascontiguousarray(x[b][order])
        m["pos"] = np.ascontiguousarray(positions[b][order]).reshape(1, S)
        m["cmats"] = _const_mats(role)
        in_maps.append(m)
    return in_maps, orders


def kernel(**inputs):
    in_maps, orders = _prep_inputs(**inputs)
    if "nc" not in _NC_CACHE:
        _NC_CACHE["nc"] = build_program()
    nc = _NC_CACHE["nc"]
    res = run_bass_kernel_spmd(nc, in_maps, core_ids=list(range(8)))
    out = np.zeros((4, S, D), np.float32)
    for c in range(8):
        b, rows = orders[c]
        out[b][rows] = res.results[c]["out"]
    return out
```

```python
import contextlib
import os
import numpy as np
import concourse.bass as bass
import concourse.mybir as mybir
from concourse.bass_utils import run_bass_kernel_spmd

F32 = mybir.dt.float32
BF16 = mybir.dt.bfloat16
I32 = mybir.dt.int32
AF = mybir.ActivationFunctionType
ALU = mybir.AluOpType

D = 1024
S = 4096
HID = 2816
NHC = HID // 128
INW = 3944
EPS = 1e-6
NEG = -30000.0
C_QLAT, C_KVLAT, C_KROPE, C_FQ, C_FK, C_FV, C_FL, C_GM, C_GF = 0, 192, 320, 352, 864, 1376, 1888, 1896, 2920


class Tok:
    __slots__ = ("name", "last_w", "readers")

    def __init__(self, name):
        self.name = name
        self.last_w = None
        self.readers = {}


class Op:
    __slots__ = ("eng", "fn", "reads", "writes", "key", "grp", "deps", "sig", "idx", "semval", "sem", "grp_end")

    def __init__(self, eng, fn, reads, writes, key, grp):
        self.eng = eng
        self.fn = fn
        self.reads = reads
        self.writes = writes
        self.key = key
        self.grp = grp
        self.deps = ()
        self.sig = False
        self.sem = None
        self.semval = None
        self.grp_end = None


class Tile:
    def __init__(self, t, toks):
        self.t = t
        self.toks = tuple(toks)

    def __getitem__(self, k):
        return self.t[k]


def toks_of(*tiles):
    out = []
    for t in tiles:
        if isinstance(t, Tile):
            out.extend(t.toks)
        elif isinstance(t, Tok):
            out.append(t)
        else:
            out.extend(toks_of(*t))
    return out


class Prog:
    ENGS = ("pe", "act", "dve", "pool", "sp")

    def __init__(self, nc):
        self.nc = nc
        self.ops = []
        self.stack = contextlib.ExitStack()
        self.ntok = 0

    def tok(self, name=None):
        self.ntok += 1
        return Tok(name or f"t{self.ntok}")

    def op(self, eng, fn, reads=(), writes=(), key=None, grp=None):
        o = Op(eng, fn, tuple(toks_of(*reads)), tuple(toks_of(*writes)), key, grp)
        o.idx = len(self.ops)
        self.ops.append(o)
        return o

    def dma(self, eng, out, in_, reads=(), writes=(), key=None, grp=None):
        if isinstance(key, Tile):
            key = key.toks[0]
        assert key is not None
        return self.op(eng, lambda e: e.dma_start(out=out, in_=in_), reads, writes, key=key, grp=grp)

    def analyze(self):
        last_dma = {}
        for o in self.ops:
            deps = {}
            for t in o.reads:
                if t.last_w is not None:
                    deps[t.last_w.idx] = t.last_w
            for t in o.writes:
                if t.last_w is not None:
                    deps[t.last_w.idx] = t.last_w
                for r in t.readers.values():
                    deps[r.idx] = r
            if o.key is not None:
                pv = last_dma.get(id(o.key))
                if pv is not None and not (o.grp is not None and pv.grp == o.grp):
                    deps[pv.idx] = pv
                last_dma[id(o.key)] = o
            deps.pop(o.idx, None)
            dl = []
            for d in deps.values():
                if o.key is not None and d.key is o.key and o.grp is not None and d.grp == o.grp:
                    continue
                if d.eng == "pe" and o.eng == "pe" and d.key is None and o.key is None:
                    continue
                dl.append(d)
            o.deps = dl
            if o.key is not None:
                o.sig = True
            for d in dl:
                d.sig = True
            rk = o.eng if o.key is None else ("k", id(o.key))
            for t in o.reads:
                t.readers[rk] = o
            for t in o.writes:
                t.last_w = o
                t.readers = {}

    def emit(self, final_waits=()):
        nc = self.nc
        self.analyze()
        for o in final_waits:
            o.sig = True
        eng_sem = {e: self.stack.enter_context(nc.semaphore(f"s_{e}")) for e in self.ENGS}
        eng_cnt = {e: 0 for e in self.ENGS}
        by_key = {}
        for o in self.ops:
            if o.key is not None:
                by_key.setdefault(id(o.key), []).append(o)
        nks = 0
        for lst in by_key.values():
            if not any(o.sig for o in lst):
                continue
            sem = self.stack.enter_context(nc.semaphore(f"k{nks}"))
            nks += 1
            for n, o in enumerate(lst):
                o.sem = sem
                o.semval = 16 * (n + 1)
            i = len(lst) - 1
            while i >= 0:
                o = lst[i]
                if o.grp is None:
                    o.grp_end = o.semval
                    i -= 1
                else:
                    endv, g = o.semval, o.grp
                    while i >= 0 and lst[i].grp == g:
                        lst[i].grp_end = endv
                        i -= 1
        for o in self.ops:
            if o.key is None and o.sig:
                eng_cnt[o.eng] += 1
                o.sem = eng_sem[o.eng]
                o.semval = eng_cnt[o.eng]
                o.grp_end = o.semval
        self.nsem = nks + 5
        per_eng = {e: [o for o in self.ops if o.eng == e] for e in self.ENGS}
        fin = list(final_waits)

        def run(engname, eng):
            waited = {}

            def do_waits(deps):
                need = {}
                for d in deps:
                    v, s = d.grp_end, d.sem
                    if waited.get(id(s), 0) >= v:
                        continue
                    if need.get(id(s), (None, 0))[1] < v:
                        need[id(s)] = (s, v)
                for s, v in need.values():
                    eng.wait_ge(s, v)
                    waited[id(s)] = v

            for o in per_eng[engname]:
                do_waits(o.deps)
                if o.fn is None:
                    continue
                ins = o.fn(eng)
                if o.sig and o.sem is not None:
                    ins.then_inc(o.sem, 16 if o.key is not None else 1)
            if engname == "sp" and fin:
                do_waits(fin)

        with nc.Block() as block:
            @block.tensor
            def _(e):
                run("pe", e)

            @block.scalar
            def _(e):
                run("act", e)

            @block.vector
            def _(e):
                run("dve", e)

            @block.gpsimd
            def _(e):
                run("pool", e)

            @block.sync
            def _(e):
                run("sp", e)
        self.stack.close()


class Ring:
    def __init__(self, items):
        self.items = list(items)
        self.i = 0

    def next(self):
        it = self.items[self.i % len(self.items)]
        self.i += 1
        return it


class Arena:
    def __init__(self, p, nc, name, nbytes, slot):
        self.p = p
        self.nbytes = nbytes
        self.slot = slot
        self.t = nc.alloc_sbuf_tensor(name, [128, nbytes // 2], BF16)
        self.toks = [p.tok(f"{name}{i}") for i in range((nbytes + slot - 1) // slot)]

    def view(self, off, shape, dtype):
        esz = 2 if dtype == BF16 else 4
        n = int(np.prod(shape))
        nb = n * esz
        assert off % 4 == 0 and off + nb <= self.nbytes, (off, nb, self.nbytes)
        ap = self.t[:, off // 2:(off + nb) // 2]
        if dtype != BF16:
            ap = ap.bitcast(dtype)
        if len(shape) == 2:
            ap = ap.rearrange("p (a b) -> p a b", a=shape[0])
        elif len(shape) == 3:
            ap = ap.rearrange("p (a b c) -> p a b c", a=shape[0], b=shape[1])
        elif len(shape) == 4:
            ap = ap.rearrange("p (a b c d) -> p a b c d", a=shape[0], b=shape[1], c=shape[2])
        toks = self.toks[off // self.slot:(off + nb - 1) // self.slot + 1]
        t = Tile(ap, toks)
        t.arena, t.off = self, off
        return t

    def toks_range(self, off, nb):
        return self.toks[off // self.slot:(off + nb - 1) // self.slot + 1]


class Carver:
    def __init__(self, arena, base=0):
        self.arena = arena
        self.off = base

    def take(self, shape, dtype):
        esz = 2 if dtype == BF16 else 4
        nb = int(np.prod(shape)) * esz
        nb_al = (nb + 63) // 64 * 64
        if nb >= self.arena.slot:
            self.off = (self.off + self.arena.slot - 1) // self.arena.slot * self.arena.slot
        v = self.arena.view(self.off, shape, dtype)
        self.off += nb_al
        return v


def build_program(debug=False, stop_after=99):
    nc = bass.Bass("TRN2", target_bir_lowering=False)
    p = Prog(nc)

    def din(name, shape, dt=F32):
        return nc.dram_tensor(name, list(shape), dt, kind="ExternalInput").ap()

    skind = "ExternalOutput" if debug else "Internal"

    def dscr(name, shape, dt):
        return nc.dram_tensor(name, list(shape), dt, kind=skind).ap(), p.tok(name)

    x_d = din("x", [S, D])
    pos_d = din("pos", [1, S], I32)
    wg_d = [din("w_g1", [D, HID]), din("w_g2", [D, HID])]
    wu_d = [din("w_u1", [D, HID]), din("w_u2", [D, HID])]
    wd_d = [din("w_d1", [HID, D]), din("w_d2", [HID, D])]
    win_d = din("w_in", [D, INW])
    wqb_d = din("w_qb", [192, 768])
    wkvb_d = din("w_kvb", [128, 1024])
    wa_d = din("w_a", [512, D])
    wb_d = din("w_b", [512, D])
    wo_d = din("w_o", [D, D])
    cols_d = din("cols", [128, 64])
    cm_d = din("cmats", [128, 9, 128])
    out_d = nc.dram_tensor("out", [2048, D], F32, kind="ExternalOutput").ap()

    x1_s, t_x1s = dscr("x1_scr", [2048, D], F32)
    h2T_s, t_h2Ts = dscr("h2T_scr", [8, 128, S], BF16)
    Km_s, t_Ks = dscr("Km_scr", [8, 96, S], BF16)
    Kf_s, _ = dscr("Kf_scr", [8, 64, S], BF16)
    Qm_s, t_Qs = dscr("Qm_scr", [8, 96, 2048], BF16)
    Qf_s, _ = dscr("Qf_scr", [8, 64, 2048], BF16)
    LF_s, t_LFs = dscr("LF_scr", [8, S], F32)
    K_s = [Km_s, Kf_s]
    Q_s = [Qm_s, Qf_s]
    V_s, t_Vs = dscr("V_scr", [2, 8, 128, 32, 65], BF16)
    CK_s, t_CKs = dscr("CK_scr", [8, 3, S], BF16)
    CQ_s, t_CQs = dscr("CQ_scr", [8, 3, 2048], BF16)
    SG_s, t_SGs = dscr("SG_scr", [2, 8, 128, 2048], BF16)
    Y_s, t_Ys = dscr("Y_scr", [2, 8, 64, 2048], BF16)

    WA = Arena(p, nc, "wa", 135168, 512)
    STG = Arena(p, nc, "stg", 16384, 512)
    WK = Arena(p, nc, "wk", 56832, 1024)
    CST = Arena(p, nc, "cst", 4352, 4352)
    t_cst = CST.toks[0]
    cc = Carver(CST)
    cols = cc.take([64], F32)
    cmf = cc.take([3, 128], F32)
    cmb = cc.take([9, 128], BF16)
    negbf = cc.take([1], F32)
    epsc = cc.take([1], F32)
    onec = cc.take([1], F32)
    IDENT, ONES, B96, B128, R96, TRI, ROLE0, ROLE1, MCUM = range(9)

    PS = nc.alloc_psum_tensor("ps", [128, 4096], F32)
    tb = [p.tok(f"bank{i}") for i in range(8)]

    def bank(i, n=1):
        return Tile(PS[:, i * 512:(i + n) * 512], tb[i:i + n])

    def bank_bf(i):
        return Tile(PS[:, i * 512:(i + 1) * 512].bitcast(BF16), tb[i:i + 1])

    stg_ring = Ring([STG.view(i * 4096, [1024], F32) for i in range(4)])

    p.dma("sp", cols[:], cols_d, writes=[cols], key=t_cst, grp="c")
    st0 = stg_ring.next()
    for half, (a0, a1) in enumerate(((0, 5), (5, 9))):
        sth = stg_ring.next() if half else st0
        sthv = sth[:, 0:(a1 - a0) * 128].rearrange("p (a b) -> p a b", a=a1 - a0)
        p.dma("sp", sthv, cm_d[:, a0:a1, :], writes=[sth], key=sth)
        p.op("dve", lambda e, sthv=sthv, a0=a0, a1=a1: e.tensor_copy(out=cmb[:, a0:a1, :], in_=sthv), [sth], [cmb])
        for i_, j_ in ((0, 0), (1, 1), (2, 8)):
            if a0 <= j_ < a1:
                p.op("dve", lambda e, i_=i_, j_=j_, sthv=sthv, a0=a0: e.tensor_copy(out=cmf[:, i_, :], in_=sthv[:, j_ - a0, :]),
                     [sth], [cmf])
    FID, FONES, FMCUM = 0, 1, 2
    p.op("dve", lambda e: e.tensor_scalar(out=negbf[:], in0=cols[:, 31:32], scalar1=-1.0, scalar2=None, op0=ALU.mult),
         [cols], [negbf])
    p.op("dve", lambda e: e.memset(epsc[:], EPS), [], [epsc])
    p.op("dve", lambda e: e.memset(onec[:], 1.0), [], [onec])

    cast_engs = Ring(["dve", "act"])
    plain_engs = Ring(["pool", "dve", "act"])

    def load_w(*a_, **kw):
        for _ in load_w_gen(*a_, **kw):
            pass

    def load_w_gen(w_d, K, N, dst, gain_col0=None, engs=None, plain=None, dma_eng="sp", c_lo=0, c_hi=None):
        nkc = (K + 127) // 128
        npc = (N + 1023) // 1024
        wpc = (N + npc - 1) // npc
        for kc in range(nkc):
            rows = min(128, K - kc * 128)
            for c0 in range(0, N, wpc):
                if c0 < c_lo or (c_hi is not None and c0 >= c_hi):
                    continue
                w = min(wpc, N - c0)
                st = stg_ring.next()
                p.dma(dma_eng, st[0:rows, 0:w], w_d[kc * 128:kc * 128 + rows, c0:c0 + w], writes=[st], key=st)
                o = dst[0:rows, kc, c0:c0 + w]
                wt = dst
                if getattr(dst, "arena", None) is not None:
                    wt = dst.arena.toks_range(dst.off + (kc * N + c0) * 2, w * 2)
                if gain_col0 is not None:
                    g = cols[0:rows, gain_col0 + kc:gain_col0 + kc + 1]
                    en = (engs or cast_engs).next()
                    if en == "dve":
                        p.op("dve", lambda e, o=o, st=st, g=g, rows=rows, w=w: e.tensor_scalar(
                            out=o, in0=st[0:rows, 0:w], scalar1=g, scalar2=None, op0=ALU.mult), [st, cols], [wt])
                    else:
                        p.op("act", lambda e, o=o, st=st, g=g, rows=rows, w=w: e.activation(
                            out=o, in_=st[0:rows, 0:w], func=AF.Copy, scale=g), [st, cols], [wt])
                else:
                    en = (plain or plain_engs).next()
                    if en == "act":
                        p.op("act", lambda e, o=o, st=st, rows=rows, w=w: e.copy(out=o, in_=st[0:rows, 0:w]), [st], [wt])
                    else:
                        p.op(en, lambda e, o=o, st=st, rows=rows, w=w: e.tensor_copy(out=o, in_=st[0:rows, 0:w]), [st], [wt])
                yield None

    def ffn_views():
        wg = [WA.view(kc * 5632, [1, HID], BF16) for kc in range(8)]
        wu = [WA.view(45056 + kc * 5632, [1, HID], BF16) for kc in range(8)]
        wd = [WA.view(90112 + hc * 2048, [1, D], BF16) for hc in range(NHC)]
        return wg, wu, wd

    def load_ffn_gen(l, part="all", hcs=None, **kw):
        wg, wu, wd = ffn_views()
        g0 = 0 if l == 0 else 16
        if part in ("all", "gu"):
            for kc in range(8):
                yield from load_w_gen(wg_d[l][kc * 128:(kc + 1) * 128, :], 128, HID, wg[kc], gain_col0=g0 + kc, **kw)
                yield from load_w_gen(wu_d[l][kc * 128:(kc + 1) * 128, :], 128, HID, wu[kc], gain_col0=g0 + kc, **kw)
        if part in ("all", "d"):
            for hc in (hcs if hcs is not None else range(NHC)):
                yield from load_w_gen(wd_d[l][hc * 128:(hc + 1) * 128, :], 128, D, wd[hc], **kw)

    tr_banks = Ring([6, 7])

    def rms1(xt, hb, st):
        p.op("act", lambda e: e.activation(out=hb[:], in_=xt[:], func=AF.Square, accum_out=st[:, 0:1]), [xt], [hb, st])
        p.op("act", lambda e: e.activation(out=st[:, 1:2], in_=st[:, 0:1], func=AF.Ln, bias=epsc[:], scale=1.0 / D),
             [st, epsc], [st])
        p.op("act", lambda e: e.activation(out=st[:, 2:3], in_=st[:, 1:2], func=AF.Exp, scale=-0.5), [st], [st])
        p.op("dve", lambda e: e.tensor_scalar(out=hb[:], in0=xt[:], scalar1=st[:, 2:3], scalar2=None, op0=ALU.mult),
             [xt, st], [hb])

    def rms2(hb, hT, s):
        pb = bank_bf(tr_banks.next())
        for kc in range(8):
            p.op("pe", lambda e, kc=kc: e.transpose(pb[:, kc * 128:(kc + 1) * 128], hb[:, kc * 128:(kc + 1) * 128],
                                                     cmb[:, IDENT, :]), [hb, cmb], [pb])
        p.op("dve", lambda e: e.tensor_copy(out=hT[:, :, s * 128:(s + 1) * 128],
                                            in_=pb[:, :].rearrange("p (k t) -> p k t", k=8)), [pb], [hT])

    gu_banks = Ring([0, 1, 2])
    o_banks = Ring([3, 4, 5])

    def ffn_gateup(hT, actT, sg_ring, wg, wu, hcs):
        for hc in hcs:
            pb = bank(gu_banks.next())
            for which, wlist in ((0, wg), (1, wu)):
                for kc in range(8):
                    p.op("pe", lambda e, which=which, wl=wlist, kc=kc, hc=hc, pb=pb: e.matmul(
                        pb[:, which * 256:(which + 1) * 256], lhsT=wl[kc][:, 0, hc * 128:(hc + 1) * 128],
                        rhs=hT[:, kc, :], start=(kc == 0), stop=(kc == 7)), [wlist[kc], hT], [pb])
            sg = sg_ring.next()
            p.op("act", lambda e, pb=pb, sg=sg: e.activation(out=sg[:], in_=pb[:, 0:256], func=AF.Silu), [pb], [sg])
            p.op("dve", lambda e, pb=pb, sg=sg, hc=hc: e.tensor_tensor(out=actT[:, hc, :], in0=pb[:, 256:512], in1=sg[:],
                                                                        op=ALU.mult), [pb, sg], [actT])

    def ffn_down(xs, actT, wd):
        for s in range(2):
            for dh in range(2):
                pb = bank(o_banks.next())
                for hc in range(NHC):
                    p.op("pe", lambda e, s=s, dh=dh, hc=hc, pb=pb: e.matmul(
                        pb[:, :], lhsT=actT[:, hc, s * 128:(s + 1) * 128], rhs=wd[hc][:, 0, dh * 512:(dh + 1) * 512],
                        start=(hc == 0), stop=(hc == NHC - 1)), [actT, wd[hc]], [pb])
                p.op("dve", lambda e, s=s, dh=dh, pb=pb: e.scalar_tensor_tensor(
                    out=xs[s][:, dh * 512:(dh + 1) * 512], in0=pb[:, :], scalar=0.5, in1=xs[s][:, dh * 512:(dh + 1) * 512],
                    op0=ALU.mult, op1=ALU.add), [pb, xs[s]], [xs[s]])

    def ffn_pass(n_tiles, src_ap, src_tok, post1_fn, post2_fn, with_h2):
        wg, wu, wd = ffn_views()
        wc = Carver(WK)
        xs_ring = Ring([[wc.take([D], F32), wc.take([D], F32)] for _ in range(2)])
        hb_ring = Ring([wc.take([D], BF16) for _ in range(4 if with_h2 else 2)])
        hT_ring = Ring([wc.take([8, 256], BF16) for _ in range(2)])
        actT = wc.take([NHC, 256], BF16)
        sg_ring = Ring([wc.take([256], BF16) for _ in range(3)])
        st_ring = Ring([wc.take([4], F32) for _ in range(8)])
        h2T_ring = Ring([wc.take([8, 256], BF16) for _ in range(2)]) if with_h2 else None

        def pre0(t):
            xs = xs_ring.next()
            for s in range(2):
                r0 = t * 256 + s * 128
                p.dma("sp", xs[s][:], src_ap[r0:r0 + 128, :], reads=([src_tok] if src_tok else []), writes=[xs[s]], key=xs[s])
            return xs

        def pre1(xs):
            hbs = [hb_ring.next(), hb_ring.next()]
            for s in range(2):
                rms1(xs[s], hbs[s], st_ring.next())
            return hbs

        def pre2(hbs):
            hT = hT_ring.next()
            for s in range(2):
                rms2(hbs[s], hT, s)
            return hT

        xs = pre0(0)
        hT = pre2(pre1(xs))
        pend_post = None
        for t in range(n_tiles):
            if t + 1 < n_tiles:
                xs_n = pre0(t + 1)
            ffn_gateup(hT, actT, sg_ring, wg, wu, range(0, 4))
            if pend_post is not None:
                post2_fn(*pend_post)
                pend_post = None
            if t + 1 < n_tiles:
                hbs_n = pre1(xs_n)
            ffn_gateup(hT, actT, sg_ring, wg, wu, range(4, 12))
            if t + 1 < n_tiles:
                hT_n = pre2(hbs_n)
            ffn_gateup(hT, actT, sg_ring, wg, wu, range(12, NHC))
            ffn_down(xs, actT, wd)
            pend_post = post1_fn(t, xs, hb_ring, st_ring, h2T_ring)
            if t + 1 < n_tiles:
                xs, hT = xs_n, hT_n
        if pend_post is not None:
            post2_fn(*pend_post)

    for _ in load_ffn_gen(0):
        pass

    def p1_post1(t, xs, hb_ring, st_ring, h2T_ring):
        hbs = [hb_ring.next(), hb_ring.next()]
        for s in range(2):
            r0 = t * 256 + s * 128
            if t < 8:
                p.dma("sp", x1_s[r0:r0 + 128, :], xs[s][:], reads=[xs[s]], writes=[t_x1s], key=xs[s])
            rms1(xs[s], hbs[s], st_ring.next())
        return (t, hbs, h2T_ring)

    def p1_post2(t, hbs, h2T_ring):
        h2T = h2T_ring.next()
        for s in range(2):
            rms2(hbs[s], h2T, s)
        p.dma("sp", h2T_s[:, :, t * 256:(t + 1) * 256].rearrange("k p t -> p k t"), h2T[:], reads=[h2T],
              writes=[t_h2Ts], key=h2T)

    ffn_pass(16, x_d, None, p1_post1, p1_post2, True)

    if stop_after <= 1:
        p.emit(final_waits=[o for o in p.ops if o.key is not None and (t_h2Ts in o.writes or t_x1s in o.writes)])
        return nc

    win = [WA.view(kc * 7888, [1, INW], BF16) for kc in range(8)]
    wcar = Carver(WA, base=8 * 7888)
    wqb = wcar.take([2, 768], BF16)
    load_w(wqb_d, 192, 768, wqb, gain_col0=24)
    wkvb = wcar.take([1, 1024], BF16)
    load_w(wkvb_d, 128, 1024, wkvb, gain_col0=26)
    wkn2 = wcar.take([4, 128], BF16)
    p.op("dve", lambda e: e.tensor_copy(
        out=wkn2[:, :, :].rearrange("p j (i c) -> p j i c", i=2),
        in_=wkvb[:, 0, :].rearrange("p (j i c) -> p j i c", j=4, i=2)[:, :, :, 0:64]), [wkvb], [wkn2])
    wqn2 = wcar.take([2, 4, 128], BF16)
    wqr = wcar.take([2, 2, 128], BF16)
    for kc, rows in ((0, 128), (1, 64)):
        p.op("dve", lambda e, kc=kc, rows=rows: e.tensor_copy(
            out=wqn2[0:rows, kc, :, :].rearrange("p j (i c) -> p j i c", i=2),
            in_=wqb[0:rows, kc, :].rearrange("p (j i c) -> p j i c", j=4, i=2)[:, :, :, 0:64]), [wqb], [wqn2])
        p.op("dve", lambda e, kc=kc, rows=rows: e.tensor_copy(
            out=wqr[0:rows, kc, :, :].rearrange("p g (i c) -> p g i c", i=4),
            in_=wqb[0:rows, kc, :].rearrange("p (g i c) -> p g i c", g=2, i=4)[:, :, :, 64:96]), [wqb], [wqr])

    wc = Carver(WK)
    hT_ring = Ring([wc.take([8, 512], BF16) for _ in range(2)])
    posi = wc.take([512], I32)
    ang = wc.take([512], F32)
    tmpf = wc.take([512], F32)
    tmpi = posi
    cs_ring = Ring([(wc.take([512], F32), wc.take([512], F32)) for _ in range(2)])
    wfree = Carver(WA, base=(wcar.off + 511) // 512 * 512)
    raw_ring = Ring([wfree.take([512], F32) for _ in range(5)])
    sq_ring = Ring([wfree.take([512], BF16) for _ in range(5)])
    rs_ring = Ring([wfree.take([512], F32) for _ in range(5)])
    qn_ring = Ring([wfree.take([512], BF16) for _ in range(4)])
    t1_ring = Ring([wfree.take([512], F32) for _ in range(3)])
    t2_ring = Ring([wfree.take([512], F32) for _ in range(3)])
    ob_ring = Ring([wfree.take([512], BF16) for _ in range(6)])
    vt_ring = Ring([wfree.take([8, 4, 65], BF16) for _ in range(3)])
    kvn_ring = Ring([wc.take([512], BF16) for _ in range(2)])
    qn12_ring = Ring([(wc.take([512], BF16), wc.take([512], BF16)) for _ in range(2)])
    lf_ring = Ring([wc.take([512], F32) for _ in range(1)])
    pr = Ring(list(range(8)))
    for vt in vt_ring.items:
        p.op("pool", lambda e, vt=vt: e.memset(vt[:], 1.0), [], [vt])
    dq = Ring(["sp"])
    uniq = [0]

    def win_toks(kc, col0, n):
        return WA.toks_range(win[kc].off + col0 * 2, n * 2)

    def proj_fm(pb, nrow, col0, hT):
        for kc in range(8):
            p.op("pe", lambda e, kc=kc: e.matmul(pb[0:nrow, :], lhsT=win[kc][:, 0, col0:col0 + nrow], rhs=hT[:, kc, :],
                                                  start=(kc == 0), stop=(kc == 7)), [win_toks(kc, col0, nrow), hT], [pb])

    def stage_a(pb, nrow):
        raw, sq = raw_ring.next(), sq_ring.next()
        p.op("dve", lambda e: e.tensor_copy(out=raw[0:nrow, :], in_=pb[0:nrow, :]), [pb], [raw])
        p.op("pool", lambda e: e.tensor_tensor(out=sq[0:nrow, :], in0=raw[0:nrow, :], in1=raw[0:nrow, :], op=ALU.mult),
             [raw], [sq])
        return raw, sq

    def stage_b(sqs, bmat, nrow, scale):
        pb2 = bank(pr.next())
        rs = rs_ring.next()
        for i, (sq, kr) in enumerate(sqs):
            p.op("pe", lambda e, sq=sq, kr=kr, i=i: e.matmul(pb2[0:nrow, :], lhsT=cmb[0:kr, bmat, 0:nrow], rhs=sq[0:kr, :],
                                                           start=(i == 0), stop=(i == len(sqs) - 1)), [cmb, sq], [pb2])
        p.op("act", lambda e: e.activation(out=rs[0:nrow, :], in_=pb2[0:nrow, :], func=AF.Ln, bias=epsc[0:nrow, :], scale=scale),
             [pb2, epsc], [rs])
        p.op("act", lambda e: e.activation(out=rs[0:nrow, :], in_=rs[0:nrow, :], func=AF.Exp, scale=-0.5), [rs], [rs])
        return rs

    def stage_rope(qn, nrow, Ct, St, dsts):
        pb3 = bank(pr.next())
        p.op("pe", lambda e: e.matmul(pb3[0:nrow, :], lhsT=cmb[0:nrow, R96, 0:nrow], rhs=qn[0:nrow, :], start=True, stop=True),
             [cmb, qn], [pb3])
        t1, t2, ob = t1_ring.next(), t2_ring.next(), ob_ring.next()
        p.op("pool", lambda e: e.tensor_tensor(out=t1[0:nrow, :], in0=qn[0:nrow, :], in1=Ct[0:nrow, :], op=ALU.mult), [qn, Ct], [t1])
        p.op("dve", lambda e: e.tensor_tensor(out=t2[0:nrow, :], in0=pb3[0:nrow, :], in1=St[0:nrow, :], op=ALU.mult), [pb3, St], [t2])
        p.op("pool", lambda e: e.tensor_tensor(out=ob[0:nrow, :], in0=t1[0:nrow, :], in1=t2[0:nrow, :], op=ALU.add), [t1, t2], [ob])
        for (dst_ap, dst_tok, r0, r1) in dsts:
            p.dma(dq.next(), dst_ap, ob[r0:r1, :], reads=[ob], writes=[dst_tok], key=ob, grp=f"r{uniq[0]}")
        uniq[0] += 1

    def ch_normed(mm_fn, nrow, bmat, scale, gain_col, dsts, rope=None):
        pb = bank(pr.next())
        mm_fn(pb)
        raw, sq = stage_a(pb, nrow)
        yield
        yield
        rs = stage_b([(sq, nrow)], bmat, nrow, scale)
        if rope is None:
            ob = ob_ring.next()
            p.op("dve", lambda e: e.scalar_tensor_tensor(out=ob[0:nrow, :], in0=raw[0:nrow, :],
                                                          scalar=cols[0:nrow, gain_col:gain_col + 1], in1=rs[0:nrow, :],
                                                          op0=ALU.mult, op1=ALU.mult), [raw, rs, cols], [ob])
            for (dst_ap, dst_tok, r0, r1) in dsts:
                p.dma(dq.next(), dst_ap, ob[r0:r1, :], reads=[ob], writes=[dst_tok], key=ob, grp=f"o{uniq[0]}")
            uniq[0] += 1
            return
        qn = qn_ring.next()
        p.op("dve", lambda e: e.scalar_tensor_tensor(out=qn[0:nrow, :], in0=raw[0:nrow, :],
                                                      scalar=cols[0:nrow, gain_col:gain_col + 1], in1=rs[0:nrow, :],
                                                      op0=ALU.mult, op1=ALU.mult), [raw, rs, cols], [qn])
        yield
        yield
        stage_rope(qn, nrow, rope[0], rope[1], dsts)

    def ch_kvlat(hT, kvn):
        pb = bank(pr.next())
        proj_fm(pb, 128, C_KVLAT, hT)
        raw, sq = stage_a(pb, 128)
        yield
        yield
        rs = stage_b([(sq, 128)], ONES, 128, 1.0 / 128)
        p.op("dve", lambda e: e.tensor_tensor(out=kvn[:, :], in0=raw[:, :], in1=rs[:, :], op=ALU.mult), [raw, rs], [kvn])

    def ch_qlat(hT, qn1, qn2):
        pb1 = bank(pr.next())
        proj_fm(pb1, 128, C_QLAT, hT)
        raw1, sq1 = stage_a(pb1, 128)
        pb2_ = bank(pr.next())
        proj_fm(pb2_, 64, C_QLAT + 128, hT)
        raw2, sq2 = stage_a(pb2_, 64)
        yield
        yield
        rs = stage_b([(sq1, 128), (sq2, 64)], ONES, 128, 1.0 / 192)
        p.op("dve", lambda e: e.tensor_tensor(out=qn1[:, :], in0=raw1[:, :], in1=rs[:, :], op=ALU.mult), [raw1, rs], [qn1])
        p.op("dve", lambda e: e.tensor_tensor(out=qn2[0:64, :], in0=raw2[0:64, :], in1=rs[0:64, :], op=ALU.mult), [raw2, rs], [qn2])

    def ch_simple(fn):
        fn()
        return
        yield

    def run_chains(chains):
        active = []
        for g in chains:
            active.append(g)
            for a_ in reversed(list(active)):
                try:
                    next(a_)
                except StopIteration:
                    active.remove(a_)
        while active:
            for a_ in reversed(list(active)):
                try:
                    next(a_)
                except StopIteration:
                    active.remove(a_)

    TWO_PI = float(2 * np.pi)

    def sin_table(dst, phase):
        p.op("dve", lambda e: e.tensor_scalar(out=tmpf[:, :], in0=ang[:, :], scalar1=phase, scalar2=None, op0=ALU.add),
             [ang], [tmpf])
        p.op("dve", lambda e: e.tensor_copy(out=tmpi[:, :], in_=tmpf[:, :]), [tmpf], [tmpi])
        p.op("dve", lambda e: e.tensor_copy(out=dst[:, :], in_=tmpi[:, :]), [tmpi], [dst])
        p.op("dve", lambda e: e.tensor_tensor(out=tmpf[:, :], in0=tmpf[:, :], in1=dst[:, :], op=ALU.subtract), [tmpf, dst], [tmpf])
        p.op("dve", lambda e: e.tensor_scalar(out=dst[:, :], in0=tmpf[:, :], scalar1=0.5, scalar2=None, op0=ALU.is_ge), [tmpf], [dst])
        p.op("dve", lambda e: e.tensor_tensor(out=tmpf[:, :], in0=tmpf[:, :], in1=dst[:, :], op=ALU.subtract), [tmpf, dst], [tmpf])
        p.op("dve", lambda e: e.tensor_scalar(out=dst[:, :], in0=tmpf[:, :], scalar1=-0.5, scalar2=None, op0=ALU.is_lt), [tmpf], [dst])
        p.op("dve", lambda e: e.tensor_tensor(out=tmpf[:, :], in0=tmpf[:, :], in1=dst[:, :], op=ALU.add), [tmpf, dst], [tmpf])
        p.op("act", lambda e: e.activation(out=dst[:, :], in_=tmpf[:, :], func=AF.Sin, scale=TWO_PI), [tmpf], [dst])

    def tile_prologue(tt):
        tok0 = tt * 512
        hT = hT_ring.next()
        Ct, St = cs_ring.next()
        p.dma("sp", hT[:], h2T_s[:, :, tok0:tok0 + 512].rearrange("k p t -> p k t"), reads=[t_h2Ts], writes=[hT], key=hT)
        p.dma("pool", posi[:, :], pos_d[:, tok0:tok0 + 512].broadcast_to([128, 512]), writes=[posi], key=posi)
        p.op("dve", lambda e: e.tensor_copy(out=ang[:, :], in_=posi[:, :]), [posi], [ang])
        p.op("dve", lambda e: e.tensor_scalar(out=ang[:, :], in0=ang[:, :], scalar1=cols[:, 48:49], scalar2=None, op0=ALU.mult),
             [ang, cols], [ang])
        sin_table(St, 0.0)
        sin_table(Ct, 0.25)
        return hT, Ct, St

    def tile_chains(tt, hT, Ct, St):
        own = tt < 4
        tok0 = tt * 512
        ts = slice(tok0, tok0 + 512)
        kvn = kvn_ring.next()
        qn1, qn2 = qn12_ring.next()
        vtm, vtf = vt_ring.next(), vt_ring.next()
        chains = [ch_kvlat(hT, kvn)]
        if own:
            chains.append(ch_qlat(hT, qn1, qn2))
        chains.append(ch_normed(lambda pb: proj_fm(pb, 32, C_KROPE, hT), 32, ONES, 1.0 / 32, 51,
                                [(Km_s[h, 64:96, ts], t_Ks, 0, 32) for h in range(8)], rope=(Ct, St)))
        for pair in range(4):
            chains.append(ch_normed(lambda pb, pair=pair: proj_fm(pb, 128, C_FK + pair * 128, hT), 128, B128, 1.0 / 64, 30,
                                    [(Kf_s[2 * pair + i, :, ts], t_Ks, 64 * i, 64 * i + 64) for i in range(2)]))

        def flogit():
            pb = bank(pr.next())
            proj_fm(pb, 8, C_FL, hT)
            lf_t = lf_ring.next()
            p.op("act", lambda e: e.copy(out=lf_t[0:8, :], in_=pb[0:8, :]), [pb], [lf_t])
            p.dma("sp", LF_s[:, ts], lf_t[0:8, :], reads=[lf_t], writes=[t_LFs], key=lf_t)
        chains.append(ch_simple(flogit))

        def vfox(bl):
            pb = bank(pr.next())
            for kc in range(8):
                p.op("pe", lambda e, kc=kc: e.matmul(pb[:, :], lhsT=hT[:, kc, bl * 128:(bl + 1) * 128],
                                                      rhs=win[kc][:, 0, C_FV:C_FV + 512], start=(kc == 0), stop=(kc == 7)),
                     [hT, win_toks(kc, C_FV, 512)], [pb])
            p.op("dve", lambda e: e.tensor_copy(out=vtf[:, :, bl, 0:64], in_=pb[:, :].rearrange("p (h c) -> p h c", h=8)),
                 [pb], [vtf])
        for bl in range(4):
            chains.append(ch_simple(lambda bl=bl: vfox(bl)))
        chains.append(ch_simple(lambda: p.dma("sp", V_s[1, :, :, tt * 4:(tt + 1) * 4, :].rearrange("h p b c -> p h b c"), vtf[:],
                                              reads=[vtf], writes=[t_Vs], key=vtf)))
        if own:
            for pair in range(4):
                chains.append(ch_normed(lambda pb, pair=pair: proj_fm(pb, 128, C_FQ + pair * 128, hT), 128, B128, 1.0 / 64, 29,
                                        [(Qf_s[2 * pair + i, :, ts], t_Qs, 64 * i, 64 * i + 64) for i in range(2)]))
        for pair in range(4):
            def mm(pb, pair=pair):
                p.op("pe", lambda e: e.matmul(pb[:, :], lhsT=wkn2[:, pair, :], rhs=kvn[:, :], start=True, stop=True), [wkn2, kvn], [pb])
            chains.append(ch_normed(mm, 128, B128, 1.0 / 64, 28,
                                    [(Km_s[2 * pair + i, 0:64, ts], t_Ks, 64 * i, 64 * i + 64) for i in range(2)]))

        def vmla(bl):
            pb = bank(pr.next())
            p.op("pe", lambda e: e.matmul(pb[:, :].rearrange("p (h c) -> p h c", h=8), lhsT=kvn[:, bl * 128:(bl + 1) * 128],
                                          rhs=wkvb[:, 0, :].rearrange("p (h c) -> p h c", h=8)[:, :, 64:128], start=True, stop=True),
                 [kvn, wkvb], [pb])
            p.op("act", lambda e: e.copy(out=vtm[:, :, bl, 0:64], in_=pb[:, :].rearrange("p (h c) -> p h c", h=8)), [pb], [vtm])
        for bl in range(4):
            chains.append(ch_simple(lambda bl=bl: vmla(bl)))
        chains.append(ch_simple(lambda: p.dma("sp", V_s[0, :, :, tt * 4:(tt + 1) * 4, :].rearrange("h p b c -> p h b c"), vtm[:],
                                              reads=[vtm], writes=[t_Vs], key=vtm)))
        if own:
            def gate(g, c0, dc):
                pb = bank(pr.next())
                proj_fm(pb, 128, c0 + dc * 128, hT)
                ob = ob_ring.next()
                p.op("act", lambda e: e.activation(out=ob[:, :], in_=pb[:, :], func=AF.Sigmoid,
                                                   bias=cols[:, 32 + 8 * g + dc:33 + 8 * g + dc]), [pb, cols], [ob])
                p.dma(dq.next(), SG_s[g, dc, :, ts], ob[:, :], reads=[ob], writes=[t_SGs], key=ob)
            gl = [(g, c0, dc) for g, c0 in ((0, C_GM), (1, C_GF)) for dc in range(8)]
            qch = []
            for pair in range(4):
                def mmq(pb, pair=pair):
                    p.op("pe", lambda e: e.matmul(pb[:, :], lhsT=wqn2[:, 0, pair, :], rhs=qn1[:, :], start=True, stop=False),
                         [wqn2, qn1], [pb])
                    p.op("pe", lambda e: e.matmul(pb[:, :], lhsT=wqn2[0:64, 1, pair, :], rhs=qn2[0:64, :], start=False, stop=True),
                         [wqn2, qn2], [pb])
                qch.append(ch_normed(mmq, 128, B128, 1.0 / 64, 27,
                                     [(Qm_s[2 * pair + i, 0:64, ts], t_Qs, 64 * i, 64 * i + 64) for i in range(2)]))
            for g4 in range(2):
                def mmr(pb, g4=g4):
                    p.op("pe", lambda e: e.matmul(pb[:, :], lhsT=wqr[:, 0, g4, :], rhs=qn1[:, :], start=True, stop=False),
                         [wqr, qn1], [pb])
                    p.op("pe", lambda e: e.matmul(pb[:, :], lhsT=wqr[0:64, 1, g4, :], rhs=qn2[0:64, :], start=False, stop=True),
                         [wqr, qn2], [pb])
                qch.append(ch_normed(mmr, 128, B96, 1.0 / 32, 50,
                                     [(Qm_s[4 * g4 + i, 64:96, ts], t_Qs, 32 * i, 32 * i + 32) for i in range(4)], rope=(Ct, St)))
            chains.extend(qch)
            for a_ in gl:
                chains.append(ch_simple(lambda a_=a_: gate(*a_)))
        return chains

    nxt = tile_prologue(0)
    wq_ = Ring(["sp", "pool"])
    for c0 in range(0, INW, 986):
        for kc in range(8):
            load_w(win_d[kc * 128:(kc + 1) * 128, :], 128, INW, win[kc], gain_col0=8 + kc, c_lo=c0, c_hi=c0 + 986,
                   dma_eng=wq_.next())
    for tt in range(8):
        hT, Ct, St = nxt
        if tt + 1 < 8:
            nxt = tile_prologue(tt + 1)
        run_chains(tile_chains(tt, hT, Ct, St))

    def cumsum_gen():
        wc2 = Carver(WA, base=0)
        BA = wc2.take([S], F32)
        BB = wc2.take([S], F32)
        CSP = wc2.take([3, S], BF16)
        sm = wc2.take([256], F32)
        tT = wc2.take([8], F32)
        p.dma("sp", BA[0:8, :], LF_s[:, :], reads=[t_LFs], writes=[BA], key=BA)
        yield
        yield
        p.op("act", lambda e: e.activation(out=BA[0:8, :], in_=BA[0:8, :], func=AF.Exp, bias=negbf[0:8, :], scale=-1.0),
             [BA, negbf], [BA])
        p.op("act", lambda e: e.activation(out=BA[0:8, :], in_=BA[0:8, :], func=AF.Ln, bias=onec[0:8, :], scale=1.0), [BA, onec], [BA])
        yield
        p.op("pool", lambda e: e.memset(BB[0:8, :], 1.0), [], [BB])
        yield
        p.op("dve", lambda e: e.tensor_tensor_scan(out=BB[0:8, :], data0=BB[0:8, :], data1=BA[0:8, :], initial=0.0,
                                                   op0=ALU.mult, op1=ALU.add), [BB, BA], [BB])
        SCv = BB[0:8, :].rearrange("p (b t) -> p b t", b=32)
        yield
        p.op("dve", lambda e: e.tensor_copy(out=sm[0:8, 0:32], in_=SCv[:, :, 127]), [BB], [sm])
        yield
        p.op("dve", lambda e: e.tensor_copy(out=sm[0:8, 32:33], in_=sm[0:8, 0:1]), [sm], [sm])
        yield
        p.op("dve", lambda e: e.tensor_tensor(out=sm[0:8, 33:64], in0=sm[0:8, 1:32], in1=sm[0:8, 0:31], op=ALU.subtract), [sm], [sm])
        yield
        p.op("dve", lambda e: e.tensor_tensor(out=sm[0:8, 96:128], in0=sm[0:8, 0:32], in1=sm[0:8, 32:64], op=ALU.subtract), [sm], [sm])
        yield
        pbt = bank(7)
        yield
        p.op("pe", lambda e: e.transpose(pbt[0:32, 0:8], sm[0:8, 32:64], cmf[0:8, FID, 0:8]), [sm, cmf], [pbt])
        yield
        p.op("dve", lambda e: e.tensor_copy(out=tT[0:32, :], in_=pbt[0:32, 0:8]), [pbt], [tT])
        yield
        pbo = bank(7)
        yield
        p.op("pe", lambda e: e.matmul(pbo[0:8, 0:32], lhsT=tT[0:32, :], rhs=cmf[0:32, FMCUM, 0:32], start=True, stop=True),
             [tT, cmf], [pbo])
        p.op("dve", lambda e: e.tensor_tensor(out=sm[0:8, 64:96], in0=pbo[0:8, 0:32], in1=sm[0:8, 96:128], op=ALU.subtract),
             [pbo, sm], [sm])
        yield "pe_done"
        p.op("dve", lambda e: e.tensor_tensor(out=SCv, in0=SCv, in1=sm[0:8, 64:96].unsqueeze(2).broadcast_to([8, 32, 128]),
                                              op=ALU.add), [BB, sm], [BB])
        p.op("dve", lambda e: e.tensor_scalar(out=BB[0:8, :], in0=BB[0:8, :], scalar1=8.0, scalar2=None, op0=ALU.mult), [BB], [BB])
        yield
        yield
        p.op("dve", lambda e: e.tensor_copy(out=CSP[0:8, 0, :], in_=BB[0:8, :]), [BB], [CSP])
        yield
        p.op("dve", lambda e: e.tensor_tensor(out=BA[0:8, :], in0=BB[0:8, :], in1=CSP[0:8, 0, :], op=ALU.subtract), [BB, CSP], [BA])
        yield
        p.op("dve", lambda e: e.tensor_copy(out=CSP[0:8, 1, :], in_=BA[0:8, :]), [BA], [CSP])
        yield
        p.op("dve", lambda e: e.tensor_tensor(out=BA[0:8, :], in0=BA[0:8, :], in1=CSP[0:8, 1, :], op=ALU.subtract), [BA, CSP], [BA])
        yield
        p.op("dve", lambda e: e.tensor_copy(out=CSP[0:8, 2, :], in_=BA[0:8, :]), [BA], [CSP])
        yield
        p.dma("sp", CK_s[:, :, :], CSP[0:8, :, :], reads=[CSP], writes=[t_CKs], key=CSP)
        yield
        p.op("dve", lambda e: e.tensor_scalar(out=CSP[0:8, :, :], in0=CSP[0:8, :, :], scalar1=-1.0, scalar2=None, op0=ALU.mult),
             [CSP], [CSP])
        p.dma("sp", CQ_s[:, :, :], CSP[0:8, :, 0:2048], reads=[CSP], writes=[t_CQs], key=CSP)
        yield

    cumgen = cumsum_gen()
    if stop_after <= 2:
        for _ in cumgen:
            pass
    if stop_after <= 2:
        p.emit(final_waits=[o for o in p.ops if o.key is not None and any(
            t in o.writes for t in (t_Ks, t_Qs, t_Vs, t_CKs, t_CQs, t_SGs, t_h2Ts, t_x1s))])
        return nc

    wc = Carver(WK)
    k_ring = Ring([wc.take([S], BF16) for _ in range(2)])
    q_ring = Ring([wc.take([2048], BF16) for _ in range(2)])
    v_ring = Ring([wc.take([32, 65], BF16) for _ in range(2)])
    pt_ring = Ring([wc.take([2, 512], BF16) for _ in range(4)])
    num_ring = Ring([wc.take([512], F32) for _ in range(3)])
    rc_ring = Ring([wc.take([512], F32) for _ in range(2)])
    y_ring = Ring([wc.take([512], BF16) for _ in range(2)])
    s_tiles = Ring([bank(0, 2), bank(2, 2), bank(4, 2)])
    po_ring = Ring([bank(6), bank(7)])

    def load_head(m, h):
        kt, qt, vt = k_ring.next(), q_ring.next(), v_ring.next()
        kd = 96 if m == 0 else 64
        if m == 1:
            p.op("pool", lambda e: e.memset(kt[64:96, :], 0.0), [], [kt])
            p.op("pool", lambda e: e.memset(qt[64:96, :], 0.0), [], [qt])
            p.op("pool", lambda e: e.memset(kt[64:70, :], 1.0), [], [kt])
            p.op("pool", lambda e: e.memset(qt[64:70, :], 1.0), [], [qt])
        p.dma("sp", kt[0:kd, :], K_s[m][h, 0:kd, :], reads=[t_Ks], writes=[kt], key=kt, grp=f"k{m}{h}")
        p.dma("sp", qt[0:kd, :], Q_s[m][h, 0:kd, :], reads=[t_Qs], writes=[qt], key=qt, grp=f"q{m}{h}")
        p.dma("sp", vt[:, :, :], V_s[m, h, :, :, :], reads=[t_Vs], writes=[vt], key=vt)
        if m == 1:
            p.dma("sp", kt[67:70, :], CK_s[h, :, :], reads=[t_CKs], writes=[kt], key=kt, grp=f"k{m}{h}")
            p.dma("sp", qt[64:67, :], CQ_s[h, :, :], reads=[t_CQs], writes=[qt], key=qt, grp=f"q{m}{h}")
        return kt, qt, vt

    steps = []
    heads = [(m, h) for m in [int(c) for c in os.environ.get("ATT_M", "01")] for h in range(int(os.environ.get("ATT_H", "8")))]
    for (m, h) in heads:
        for G in range(4):
            nj = 4 * G + 4
            for j in range(nj):
                steps.append(dict(m=m, h=h, G=G, j=j, first=(j == 0), last=(j == nj - 1)))

    head_tiles = {}
    state = {}
    pending = []

    def emit_qk(st):
        m, h, G, j = st["m"], st["h"], st["G"], st["j"]
        if (m, h) not in head_tiles:
            head_tiles[(m, h)] = load_head(m, h)
        kt, qt, vt = head_tiles[(m, h)]
        kd = 96
        off = 0 if j < 4 * G else (j - 4 * G) * 128
        stile = s_tiles.next()
        st["stile"] = stile
        st["off"] = off
        q0 = 512 * G + off
        for side in range(2):
            kb = j + 16 * side
            diag = j >= 4 * G
            p.op("pe", lambda e, side=side, kb=kb, diag=diag: e.matmul(
                stile[:, side * 512 + off:(side + 1) * 512], lhsT=kt[0:kd, kb * 128:(kb + 1) * 128],
                rhs=qt[0:kd, q0:512 * G + 512], start=True, stop=(not diag)), [kt, qt], [stile])
            if diag:
                mk = TRI if side == 0 else (ROLE0 if j % 2 == 0 else ROLE1)
                p.op("pe", lambda e, side=side, mk=mk: e.matmul(
                    stile[:, side * 512 + off:side * 512 + off + 128], lhsT=cmb[:, IDENT, :], rhs=cmb[:, mk, :],
                    start=False, stop=True), [cmb], [stile])

    def emit_exp_pv(st):
        m, h, G, j = st["m"], st["h"], st["G"], st["j"]
        kt, qt, vt = head_tiles[(m, h)]
        stile, off = st["stile"], st["off"]
        first, last = st["first"], st["last"]
        pt = pt_ring.next()
        scale = 96 ** -0.5 if m == 0 else 0.125
        sv = stile[:, :].rearrange("p (s q) -> p s q", s=2)
        p.op("act", lambda e: e.activation(out=pt[:, :, off:512], in_=sv[:, :, off:512], func=AF.Exp, scale=scale), [stile], [pt])
        if first:
            state["po"] = po_ring.next()
        po = state["po"]
        for side in range(2):
            kb = j + 16 * side
            p.op("pe", lambda e, side=side, kb=kb: e.matmul(
                po[0:65, off:512], lhsT=vt[:, kb, :], rhs=pt[:, side, off:512],
                start=(first and side == 0), stop=(last and side == 1)), [vt, pt], [po])
        if last:
            num, y = num_ring.next(), y_ring.next()
            p.op("dve", lambda e: e.tensor_copy(out=num[0:65, :], in_=po[0:65, :]), [po], [num])

            def norm():
                p.op("pe", lambda e: e.matmul(po[0:64, 0:512], lhsT=cmf[64:65, FONES, 0:64], rhs=num[64:65, :], start=True, stop=True),
                     [cmf, num], [po])
                rc = rc_ring.next()
                p.op("dve", lambda e: e.reciprocal(out=rc[0:64, :], in_=po[0:64, 0:512]), [po], [rc])
                p.op("dve", lambda e: e.tensor_tensor(out=y[0:64, :], in0=num[0:64, :], in1=rc[0:64, :], op=ALU.mult),
                     [rc, num], [y])
                p.dma("sp", Y_s[m, h, :, 512 * G:512 * G + 512], y[0:64, :], reads=[y], writes=[t_Ys], key=y)
            pending.append([2, norm])

    wab = [WA.view(90112, [4, D], BF16), WA.view(98304, [4, D], BF16)]
    wo = WA.view(106496, [8, D], BF16)

    def w3_all():
        kw = dict(engs=Ring(["dve"]), plain=Ring(["pool"]), dma_eng="pool")
        yield from load_w_gen(wa_d, 512, D, wab[0], plain=Ring(["pool"]), dma_eng="pool")
        yield from load_w_gen(wb_d, 512, D, wab[1], plain=Ring(["pool"]), dma_eng="pool")
        yield from load_w_gen(wo_d, D, D, wo, plain=Ring(["pool"]), dma_eng="pool")
        yield from load_ffn_gen(1, part="gu", **kw)
        yield from load_ffn_gen(1, part="d", hcs=range(16, NHC), **kw)
    w3gen = w3_all()
    if stop_after > 2:
        while next(cumgen) != "pe_done":
            pass
    LOOK = 2
    for i in range(min(LOOK, len(steps))):
        emit_qk(steps[i])
    for i, st in enumerate(steps):
        if i + LOOK < len(steps):
            emit_qk(steps[i + LOOK])
        for pe_ in list(pending):
            pe_[0] -= 1
            if pe_[0] <= 0:
                pe_[1]()
                pending.remove(pe_)
        emit_exp_pv(st)
        if st["first"] and st["G"] == 0:
            hi = heads.index((st["m"], st["h"]))
            if hi + 1 < len(heads) and heads[hi + 1] not in head_tiles:
                head_tiles[heads[hi + 1]] = load_head(*heads[hi + 1])
        if next(cumgen, "done") == "done" and i >= 48 and i % 6 == 0 and not os.environ.get("NO_W3"):
            next(w3gen, None)
    for pe_ in pending:
        pe_[1]()
    for _ in cumgen:
        pass
    if not os.environ.get("NO_W3"):
        for _ in w3gen:
            pass

    if stop_after <= 3:
        p.emit(final_waits=[o for o in p.ops if o.key is not None and any(
            t in o.writes for t in (t_Ys, t_Ks, t_Qs, t_Vs, t_CKs, t_CQs, t_SGs, t_h2Ts, t_x1s))])
        return nc

    wc = Carver(WK)
    xs_ring = Ring([[wc.take([D], F32), wc.take([D], F32)] for _ in range(3)])
    mix_ring = Ring([wc.take([8, 256], BF16) for _ in range(2)])
    sgt_ring = Ring([[wc.take([8, 256], BF16), wc.take([8, 256], BF16)] for _ in range(2)])
    sc = Carver(STG)
    ym_ring = Ring([[sc.take([4, 256], BF16), sc.take([4, 256], BF16)] for _ in range(2)])
    ta_ring = Ring([sc.take([256], BF16) for _ in range(3)])
    tb_ring = Ring([sc.take([256], BF16) for _ in range(3)])

    def pa_loads(t):
        c0 = t * 256
        xs, ym, sg = xs_ring.next(), ym_ring.next(), sgt_ring.next()
        for m in range(2):
            p.dma("sp", ym[m][:, :, :], Y_s[m, :, :, c0:c0 + 256].rearrange("(pr j) r t -> (j r) pr t", j=2),
                  reads=[t_Ys], writes=[ym[m]], key=ym[m])
            p.dma("sp", sg[m][:, :, :], SG_s[m, :, :, c0:c0 + 256].rearrange("d p t -> p d t"),
                  reads=[t_SGs], writes=[sg[m]], key=sg[m])
        for s in range(2):
            p.dma("sp", xs[s][:], x1_s[c0 + s * 128:c0 + (s + 1) * 128, :], reads=[t_x1s], writes=[xs[s]], key=xs[s])
        return xs, ym, sg

    def pa_branch(ym, sg):
        mixT = mix_ring.next()
        for dc in range(8):
            pb = bank(gu_banks.next())
            for m in range(2):
                for pr_ in range(4):
                    p.op("pe", lambda e, m=m, pr_=pr_, dc=dc, pb=pb: e.matmul(
                        pb[:, m * 256:(m + 1) * 256], lhsT=wab[m][:, pr_, dc * 128:(dc + 1) * 128], rhs=ym[m][:, pr_, :],
                        start=(pr_ == 0), stop=(pr_ == 3)), [wab[m], ym[m]], [pb])
            ta, tb_ = ta_ring.next(), tb_ring.next()
            p.op("dve", lambda e, pb=pb, ta=ta, dc=dc: e.tensor_tensor(out=ta[:, :], in0=pb[:, 0:256], in1=sg[0][:, dc, :],
                                                                    op=ALU.mult), [pb, sg[0]], [ta])
            p.op("dve", lambda e, pb=pb, tb_=tb_, dc=dc: e.tensor_tensor(out=tb_[:, :], in0=pb[:, 256:512], in1=sg[1][:, dc, :],
                                                                      op=ALU.mult), [pb, sg[1]], [tb_])
            p.op("pool", lambda e, ta=ta, tb_=tb_, dc=dc: e.tensor_tensor(out=mixT[:, dc, :], in0=ta[:, :], in1=tb_[:, :],
                                                                       op=ALU.add), [ta, tb_], [mixT])
        return mixT

    def pa_wo(t, xs, mixT):
        c0 = t * 256
        for s in range(2):
            for dh in range(2):
                pb = bank(o_banks.next())
                for dc in range(8):
                    p.op("pe", lambda e, s=s, dh=dh, dc=dc, pb=pb: e.matmul(
                        pb[:, :], lhsT=mixT[:, dc, s * 128:(s + 1) * 128], rhs=wo[:, dc, dh * 512:(dh + 1) * 512],
                        start=(dc == 0), stop=(dc == 7)), [mixT, wo], [pb])
                p.op("dve", lambda e, s=s, dh=dh, pb=pb: e.tensor_tensor(
                    out=xs[s][:, dh * 512:(dh + 1) * 512], in0=pb[:, :], in1=xs[s][:, dh * 512:(dh + 1) * 512], op=ALU.add),
                    [pb, xs[s]], [xs[s]])
            p.dma("sp", x1_s[c0 + s * 128:c0 + (s + 1) * 128, :], xs[s][:], reads=[xs[s]], writes=[t_x1s], key=xs[s])

    NA = 8
    ld = [pa_loads(0), pa_loads(1)]
    mixes = [pa_branch(ld[0][1], ld[0][2])]
    for t in range(NA):
        if t + 2 < NA:
            ld.append(pa_loads(t + 2))
        if t + 1 < NA:
            mixes.append(pa_branch(ld[t + 1][1], ld[t + 1][2]))
        pa_wo(t, ld[t][0], mixes[t])
    wq2 = Ring(["sp", "pool"])
    for hc in range(0, 16):
        for _ in load_ffn_gen(1, part="d", hcs=[hc], plain=Ring(["dve", "act"] if hc % 2 else ["act", "dve"]), dma_eng=wq2.next()):
            pass

    outs = []

    def p3_post1(t, xs, hb_ring, st_ring, h2T_ring):
        for s in range(2):
            r0 = t * 256 + s * 128
            outs.append(p.dma("sp", out_d[r0:r0 + 128, :], xs[s][:], reads=[xs[s]], key=xs[s]))
        return None

    ffn_pass(8, x1_s, t_x1s, p3_post1, None, False)
    p.emit(final_waits=outs)
    return nc


def _role_blocks(role):
    own = []
    for m in range(8):
        own += [4 * m, 4 * m + 3] if role == 0 else [4 * m + 1, 4 * m + 2]
    oth = [g for g in range(32) if g not in own]
    oth = []
    for m in range(8):
        oth += [4 * m + 1, 4 * m + 2] if role == 0 else [4 * m, 4 * m + 3]
    return own, oth


def _const_mats(role):
    cm = np.zeros((128, 9, 128), np.float32)
    cm[:, 0, :] = np.eye(128, dtype=np.float32)
    cm[:, 1, :] = 1.0
    for g in range(4):
        cm[32 * g:32 * g + 32, 2, 32 * g:32 * g + 32] = 1.0
    cm[0:64, 3, 0:64] = 1.0
    cm[64:128, 3, 64:128] = 1.0
    for g in range(4):
        for i in range(16):
            cm[32 * g + 16 + i, 4, 32 * g + i] = -1.0
            cm[32 * g + i, 4, 32 * g + 16 + i] = 1.0
    k = np.arange(128)[:, None]
    q = np.arange(128)[None, :]
    cm[:, 5, :] = np.where(k > q, NEG, 0.0)
    cm[:, 6, :] = NEG if role == 0 else 0.0
    cm[:, 7, :] = 0.0 if role == 0 else NEG
    own, oth = _role_blocks(role)
    glob = np.array(own + oth)
    cm[0:32, 8, 0:32] = (glob[:, None] < glob[None, :]).astype(np.float32)
    return cm


_NC_CACHE = {}


def _prep_inputs(x, positions, ffn1_norm, ffn1_w_gate, ffn1_w_up, ffn1_w_down, mix_norm, w_in,
                 mla_q_lat_norm, mla_w_qb, mla_kv_lat_norm, mla_w_kvb, mla_q_nope_gain,
                 mla_q_rope_gain, mla_k_nope_gain, mla_k_rope_gain, fox_q_gain, fox_k_gain,
                 fox_b_f, w_branch_mla, w_branch_fox, b_gate, w_o, ffn2_norm, ffn2_w_gate,
                 ffn2_w_up, ffn2_w_down):
    f = lambda a: np.ascontiguousarray(np.asarray(a, dtype=np.float32))
    x = f(x)
    positions = np.asarray(positions).astype(np.int32)
    cols = np.zeros((128, 64), np.float32)
    cols[:, 0:8] = f(ffn1_norm)[0].reshape(8, 128).T
    cols[:, 8:16] = f(mix_norm)[0].reshape(8, 128).T
    cols[:, 16:24] = f(ffn2_norm)[0].reshape(8, 128).T
    cols[:, 24] = f(mla_q_lat_norm)[0][0:128]
    cols[0:64, 25] = f(mla_q_lat_norm)[0][128:192]
    cols[:, 26] = f(mla_kv_lat_norm)[0]
    cols[:, 27] = np.tile(f(mla_q_nope_gain)[0], 2)
    cols[:, 28] = np.tile(f(mla_k_nope_gain)[0], 2)
    cols[:, 50] = np.tile(f(mla_q_rope_gain)[0], 4)
    cols[0:32, 51] = f(mla_k_rope_gain)[0]
    cols[:, 29] = np.tile(f(fox_q_gain)[0], 2)
    cols[:, 30] = np.tile(f(fox_k_gain)[0], 2)
    cols[0:8, 31] = f(fox_b_f)[0]
    cols[:, 32:40] = f(b_gate)[0, 0].reshape(8, 128).T
    cols[:, 40:48] = f(b_gate)[0, 1].reshape(8, 128).T
    inv_freq = (np.float32(10000.0) ** (-np.arange(16, dtype=np.float32) / np.float32(16))).astype(np.float32)
    cols[:, 48] = np.tile(inv_freq / np.float32(2 * np.pi), 8)
    shared = {
        "w_g1": f(ffn1_w_gate)[0], "w_u1": f(ffn1_w_up)[0], "w_d1": f(ffn1_w_down)[0],
        "w_g2": f(ffn2_w_gate)[0], "w_u2": f(ffn2_w_up)[0], "w_d2": f(ffn2_w_down)[0],
        "w_in": f(w_in)[0], "w_qb": f(mla_w_qb)[0], "w_kvb": f(mla_w_kvb)[0],
        "w_a": f(w_branch_mla)[0], "w_b": f(w_branch_fox)[0], "w_o": f(w_o)[0], "cols": cols,
    }
    in_maps = []
    orders = []
    for c in range(8):
        b, role = c // 2, c % 2
        own, oth = _role_blocks(role)
        order = np.concatenate([np.arange(g * 128, (g + 1) * 128) for g in own + oth])
        orders.append((b, np.concatenate([np.arange(g * 128, (g + 1) * 128) for g in own])))
        m = dict(shared)
        m["x"] = np.ascontiguousarray(x[b][order])
        m["pos"] = np.ascontiguousarray(positions[b][order]).reshape(1, S)
        m["cmats"] = _const_mats(role)
        in_maps.append(m)
    return in_maps, orders


def kernel(**inputs):
    in_maps, orders = _prep_inputs(**inputs)
    if "nc" not in _NC_CACHE:
        _NC_CACHE["nc"] = build_program()
    nc = _NC_CACHE["nc"]
    res = run_bass_kernel_spmd(nc, in_maps, core_ids=list(range(8)))
    out = np.zeros((4, S, D), np.float32)
    for c in range(8):
        b, rows = orders[c]
        out[b][rows] = res.results[c]["out"]
    return out
```
